# Optimizing a Trainium2 kernel written in Bass

```python
import jax, jax.numpy as jnp
from jax import lax
import numpy as np

D_MODEL = 1024
BATCH = 2
SEQ = 8192
DEPTH = 1

GRID_W = 64
D_MIX = D_MODEL
ATTN_WIDTH = D_MIX // 2
REC_WIDTH = D_MIX - ATTN_WIDTH
HEAD_DIM = 128
N_Q_HEADS = ATTN_WIDTH // HEAD_DIM
N_KV_HEADS = 2
KV_GROUPS = N_Q_HEADS // N_KV_HEADS
ROPE_THETA = 10000.0
ROPE_AXIS_DIM = HEAD_DIM // 2
Q_BLOCK = 128
REC_HEAD_DIM = 128
N_REC_HEADS = REC_WIDTH // REC_HEAD_DIM
REC_CHUNK = 64
D_FF = 2816
EPS = 1e-6
IN_SPLITS = (N_Q_HEADS * HEAD_DIM,
             N_KV_HEADS * HEAD_DIM,
             N_KV_HEADS * HEAD_DIM,
             REC_WIDTH,
             REC_WIDTH,
             REC_WIDTH,
             REC_WIDTH,
             REC_WIDTH)
D_IN_MIX = 512 + 256 + 256 + 5 * 512

kernel_name = "hybrid_gqa_hgrn2_macaron_encoder"


def rms_norm(x, gain):
    xf = x.astype(jnp.float32)
    y = xf * lax.rsqrt(jnp.mean(xf * xf, axis=-1, keepdims=True) + EPS)
    return (y * gain.astype(jnp.float32)).astype(x.dtype)


def swiglu_half_step(x, gain, w_in, w_out):
    h = rms_norm(x, gain)
    gate, up = jnp.split(h @ w_in, 2, axis=-1)
    return (jax.nn.silu(gate) * up) @ w_out


def axial_rope_tables(seq_len):
    rows = seq_len // GRID_W
    row_ids = jnp.repeat(jnp.arange(rows, dtype=jnp.float32), GRID_W)
    col_ids = jnp.tile(jnp.arange(GRID_W, dtype=jnp.float32), rows)
    inv_freq = ROPE_THETA ** (-jnp.arange(0, ROPE_AXIS_DIM, 2, dtype=jnp.float32) / ROPE_AXIS_DIM)
    ang_r = row_ids[:, None] * inv_freq[None, :]
    ang_c = col_ids[:, None] * inv_freq[None, :]
    return jnp.cos(ang_r), jnp.sin(ang_r), jnp.cos(ang_c), jnp.sin(ang_c)


def _rotate(xp, cos, sin):
    x1, x2 = jnp.split(xp, 2, axis=-1)
    return jnp.concatenate([x1 * cos - x2 * sin, x2 * cos + x1 * sin], axis=-1)


def apply_axial_rope(x, tables):
    cos_r, sin_r, cos_c, sin_c = (t[:, None, :] for t in tables)
    xf = x.astype(jnp.float32)
    out = jnp.concatenate([_rotate(xf[..., :ROPE_AXIS_DIM], cos_r, sin_r),
                           _rotate(xf[..., ROPE_AXIS_DIM:], cos_c, sin_c)], axis=-1)
    return out.astype(x.dtype)


def gqa_bidirectional(q, k, v, q_gain, k_gain):
    bsz, seq_len = q.shape[0], q.shape[1]
    tables = axial_rope_tables(seq_len)
    q = apply_axial_rope(rms_norm(q, q_gain), tables) * (HEAD_DIM ** -0.5)
    k = apply_axial_rope(rms_norm(k, k_gain), tables)
    n_blk = seq_len // Q_BLOCK
    qb = q.reshape(bsz, n_blk, Q_BLOCK, N_KV_HEADS, KV_GROUPS, HEAD_DIM)
    qb = jnp.moveaxis(qb, 1, 0)

    def one_block(q_blk):
        s = jnp.einsum('bqkgd,bskd->bkgqs', q_blk, k).astype(jnp.float32)
        p = jax.nn.softmax(s, axis=-1).astype(v.dtype)
        return jnp.einsum('bkgqs,bskd->bqkgd', p, v)

    o = lax.map(one_block, qb)
    return jnp.moveaxis(o, 0, 1).reshape(bsz, seq_len, ATTN_WIDTH)


def hgrn2_chunk_scan(q, k, v, log_f):
    n, seq_len, h, dk = q.shape
    dv = v.shape[-1]
    nc = seq_len // REC_CHUNK

    def to_chunks(a):
        a = a.astype(jnp.float32).reshape(n, nc, REC_CHUNK, h, a.shape[-1])
        return jnp.transpose(a, (1, 0, 3, 2, 4))

    causal = jnp.tril(jnp.ones((REC_CHUNK, REC_CHUNK), dtype=bool))[:, :, None]

    def step(state, inp):
        qc, kc, vc, gc = inp
        b = jnp.cumsum(gc, axis=-2)
        inter = jnp.einsum('nhcd,nhde->nhce', qc * jnp.exp(b), state)
        diff = b[..., :, None, :] - b[..., None, :, :]
        decay = jnp.exp(jnp.where(causal, diff, -jnp.inf))
        scores = jnp.einsum('nhtd,nhtsd,nhsd->nhts', qc, decay, kc)
        intra = jnp.einsum('nhts,nhse->nhte', scores, vc)
        b_last = b[..., -1:, :]
        new_state = jnp.exp(b_last[..., 0, :])[..., None] * state + \
            jnp.einsum('nhsd,nhse->nhde', kc * jnp.exp(b_last - b), vc)
        return new_state, inter + intra

    s0 = jnp.zeros((n, h, dk, dv), jnp.float32)
    _, o = lax.scan(step, s0, (to_chunks(q), to_chunks(k), to_chunks(v), to_chunks(log_f)))
    o = jnp.transpose(o, (1, 0, 3, 2, 4)).reshape(n, seq_len, h, dv)
    return o.astype(v.dtype)


def hgrn2_bidirectional(q, zf_fwd, zf_bwd, i, g, lb, out_gain):
    bsz, seq_len = q.shape[0], q.shape[1]
    shp = (bsz, seq_len, N_REC_HEADS, REC_HEAD_DIM)

    def gates(z, lower):
        f = lower + (1.0 - lower) * jax.nn.sigmoid(z.astype(jnp.float32))
        return jnp.log(f).reshape(shp), (1.0 - f).reshape(shp)

    logf_fwd, k_fwd = gates(zf_fwd, lb[0])
    logf_bwd, k_bwd = gates(zf_bwd, lb[1])
    q4, i4 = q.reshape(shp), i.reshape(shp)
    flip = lambda a: jnp.flip(a, axis=1)
    q_dir = jnp.concatenate([q4, flip(q4)], axis=0)
    k_dir = jnp.concatenate([k_fwd, flip(k_bwd)], axis=0)
    v_dir = jnp.concatenate([i4, flip(i4)], axis=0)
    g_dir = jnp.concatenate([logf_fwd, flip(logf_bwd)], axis=0)
    o_dir = hgrn2_chunk_scan(q_dir, k_dir, v_dir, g_dir)
    o = o_dir[:bsz] + flip(o_dir[bsz:])
    o = rms_norm(o, out_gain).reshape(bsz, seq_len, REC_WIDTH)
    return o * jax.nn.silu(g)


def hybrid_mixer(x, layer, norm_gain, w_in, q_gain, k_gain, attn_out_gain, rec_lb_logits,
                 rec_out_gain, w_out):
    bsz, seq_len = x.shape[0], x.shape[1]
    h = rms_norm(x, norm_gain)
    proj = h @ w_in
    offsets = [int(o) for o in np.cumsum(IN_SPLITS)[:-1]]
    aq, ak, av, rq, rf_fwd, rf_bwd, ri, rg = jnp.split(proj, offsets, axis=-1)
    attn = gqa_bidirectional(aq.reshape(bsz, seq_len, N_Q_HEADS, HEAD_DIM),
                             ak.reshape(bsz, seq_len, N_KV_HEADS, HEAD_DIM),
                             av.reshape(bsz, seq_len, N_KV_HEADS, HEAD_DIM), q_gain, k_gain)
    attn = rms_norm(attn, attn_out_gain)
    lb_all = jnp.cumsum(jax.nn.softmax(rec_lb_logits.astype(jnp.float32), axis=1), axis=1)
    lb = lb_all[:, layer]
    rec = hgrn2_bidirectional(rq, rf_fwd, rf_bwd, ri, rg, lb, rec_out_gain)
    return jnp.concatenate([attn, rec.astype(attn.dtype)], axis=-1) @ w_out


def setup_inputs(seed: int = 0) -> dict:
    key = jax.random.key(seed)
    ks = jax.random.split(key, 16)
    nrm = lambda k, shape, s: jax.random.normal(k, shape, jnp.float32) * s
    gain = lambda k, shape: 1.0 + 0.02 * jax.random.normal(k, shape, jnp.float32)
    return {
        "x": nrm(ks[0], (BATCH, SEQ, D_MODEL), 1.0),
        "ffn1_norm": gain(ks[1], (DEPTH, D_MODEL)),
        "ffn1_w_in": nrm(ks[2], (DEPTH, D_MODEL, 2 * D_FF), D_MODEL ** -0.5),
        "ffn1_w_out": nrm(ks[3], (DEPTH, D_FF, D_MODEL), D_FF ** -0.5),
        "mix_norm": gain(ks[4], (DEPTH, D_MODEL)),
        "w_in_mix": nrm(ks[5], (DEPTH, D_MODEL, D_IN_MIX), D_MODEL ** -0.5),
        "attn_q_norm": gain(ks[6], (DEPTH, HEAD_DIM)),
        "attn_k_norm": gain(ks[7], (DEPTH, HEAD_DIM)),
        "attn_out_norm": gain(ks[8], (DEPTH, ATTN_WIDTH)),
        "rec_lb_logits": nrm(ks[9], (2, DEPTH + 1, REC_WIDTH), 0.5),
        "rec_out_norm": gain(ks[10], (DEPTH, REC_HEAD_DIM)),
        "w_out_mix": nrm(ks[11], (DEPTH, D_MIX, D_MODEL), D_MIX ** -0.5),
        "ffn2_norm": gain(ks[12], (DEPTH, D_MODEL)),
        "ffn2_w_in": nrm(ks[13], (DEPTH, D_MODEL, 2 * D_FF), D_MODEL ** -0.5),
        "ffn2_w_out": nrm(ks[14], (DEPTH, D_FF, D_MODEL), D_FF ** -0.5),
        "final_norm": gain(ks[15], (DEPTH, D_MODEL)),
    }


def reference(x, ffn1_norm, ffn1_w_in, ffn1_w_out, mix_norm, w_in_mix, attn_q_norm, attn_k_norm,
              attn_out_norm, rec_lb_logits, rec_out_norm, w_out_mix, ffn2_norm, ffn2_w_in,
              ffn2_w_out, final_norm):
    for l in range(DEPTH):
        x = x + 0.5 * swiglu_half_step(x, ffn1_norm[l], ffn1_w_in[l], ffn1_w_out[l])
        x = x + hybrid_mixer(x, l, mix_norm[l], w_in_mix[l], attn_q_norm[l], attn_k_norm[l],
                             attn_out_norm[l], rec_lb_logits, rec_out_norm[l], w_out_mix[l])
        x = x + 0.5 * swiglu_half_step(x, ffn2_norm[l], ffn2_w_in[l], ffn2_w_out[l])
        x = rms_norm(x, final_norm[l])
    return x
```

```python
import contextlib
import os
import numpy as np
import concourse.bass as bass
import concourse.mybir as mybir
from concourse.bass_utils import run_bass_kernel_spmd

F32 = mybir.dt.float32
BF16 = mybir.dt.bfloat16
AF = mybir.ActivationFunctionType
ALU = mybir.AluOpType
AX = mybir.AxisListType

NCORES = 8
D = 1024
T = 2048
NT = 16
NQ = 4
DFF = 2816
NG = 11
DMIX_IN = 3584
EPS = 1e-6
C = 64
NCH = T // C
SW = 4 * 2 * 128 + 8


class Res:
    __slots__ = ("name", "w", "r", "dsem", "dcnt")

    def __init__(self, name):
        self.name = name
        self.w = None
        self.r = {}
        self.dsem = None
        self.dcnt = 0


class Sched:
    def __init__(self, nc, stack, ndummy=8):
        self.nc = nc
        self.stack = stack
        self.dummy = [stack.enter_context(nc.semaphore("dummy%d" % i)) for i in range(ndummy)]
        self.eng = {"pe": nc.tensor, "act": nc.scalar, "dve": nc.vector,
                    "pool": nc.gpsimd, "sp": nc.sync}
        self.ops = {e: [] for e in self.eng}
        self.sem = {e: stack.enter_context(nc.semaphore("sem_" + e)) for e in self.eng}
        self.cnt = {e: 0 for e in self.eng}
        self.pending = {e: False for e in self.eng}
        self.waited = {e: {} for e in self.eng}
        self.nsem = 0
        self.dres = []

    def res(self, name):
        return Res(name)

    def _need(self, e, ev):
        if ev is None:
            return
        sem, val = ev
        k = sem.num
        if k == self.sem[e].num and val > self.cnt[e]:
            return
        if self.waited[e].get(k, 0) >= val:
            return
        self.waited[e][k] = val
        self.ops[e].append(lambda eng, sem=sem, val=val: eng.wait_ge(sem, val))

    def _deps(self, e, reads, writes):
        for R in reads:
            self._need(e, R.w)
        for R in writes:
            self._need(e, R.w)
            for ev in list(R.r.values()):
                self._need(e, ev)

    def _mark(self, ev, reads, writes):
        sem, val = ev
        for R in reads:
            old = R.r.get(sem.num)
            if old is None or old[1] < val:
                R.r[sem.num] = ev
        for R in writes:
            R.w = ev
            R.r = {}

    def op(self, e, fn, reads=(), writes=(), inc=True):
        self._deps(e, reads, writes)
        sem = self.sem[e]
        if inc:
            self.cnt[e] += 1
            tick = self.cnt[e]
            self.pending[e] = False
            self.ops[e].append(lambda eng, fn=fn, sem=sem: fn(eng).then_inc(sem, 1))
        else:
            tick = self.cnt[e] + 1
            self.pending[e] = True
            self.ops[e].append(lambda eng, fn=fn: fn(eng))
        self._mark((sem, tick), reads, writes)

    def _dsem(self, R, pre):
        if R.dsem is None:
            R.dsem = self.stack.enter_context(self.nc.semaphore("%s%d_%s" % (pre, self.nsem, R.name)))
            self.nsem += 1
            self.dres.append(R)

    def dma(self, q, out, in_, reads=(), writes=(), track=None):
        R = track if track is not None else (writes[0] if writes else reads[0])
        self._dsem(R, "d")
        for Rr in reads:
            self._need(q, Rr.w)
        for Rw in writes:
            if not (Rw.w is not None and Rw.dsem is not None and Rw.w[0].num == Rw.dsem.num):
                self._need(q, Rw.w)
            for ev in list(Rw.r.values()):
                self._need(q, ev)
        R.dcnt += 16
        dsem = R.dsem
        self.ops[q].append(lambda eng, out=out, in_=in_, dsem=dsem:
                           eng.dma_start(out=out, in_=in_).then_inc(dsem, 16))
        ev = (dsem, R.dcnt)
        for Rr in reads:
            old = Rr.r.get(dsem.num)
            if old is None or old[1] < ev[1]:
                Rr.r[dsem.num] = ev
        for Rw in writes:
            Rw.w = ev
        return ev

    def custom(self, e, fn, R, incval, reads=(), writes=()):
        self._dsem(R, "c")
        self._deps(e, reads, writes)
        R.dcnt += incval
        dsem = R.dsem
        self.ops[e].append(lambda eng, fn=fn, dsem=dsem, incval=incval: fn(eng).then_inc(dsem, incval))
        ev = (dsem, R.dcnt)
        self._mark(ev, reads, writes)
        return ev

    def barrier(self):
        for e in self.eng:
            assert not self.pending[e]
        for e in self.eng:
            for e2 in self.eng:
                if self.cnt[e2] > 0:
                    self._need(e, (self.sem[e2], self.cnt[e2]))
            for R in self.dres:
                if R.dsem.name.startswith("c"):
                    continue
                self._need(e, (R.dsem, R.dcnt))

    def emit(self):
        nc = self.nc
        for e in self.eng:
            assert not self.pending[e], e
        ops = self.ops
        with nc.Block() as block:
            @block.tensor
            def _(eng):
                for f in ops["pe"]:
                    f(eng)

            @block.scalar
            def _(eng):
                for f in ops["act"]:
                    f(eng)

            @block.vector
            def _(eng):
                for f in ops["dve"]:
                    f(eng)

            @block.gpsimd
            def _(eng):
                for f in ops["pool"]:
                    f(eng)

            @block.sync
            def _(eng):
                for f in ops["sp"]:
                    f(eng)


def build(debug=False, stop_after=None):
    nc = bass.Bass("TRN2", target_bir_lowering=False)
    dt_in = lambda n, shp: nc.dram_tensor(n, shp, F32, kind="ExternalInput").ap()
    x_d = dt_in("x", [T, D])
    w_in1 = dt_in("ffn1_w_in", [D, 2 * DFF])
    w_out1 = dt_in("ffn1_w_out", [DFF, D])
    w_in2 = dt_in("ffn2_w_in", [D, 2 * DFF])
    w_out2 = dt_in("ffn2_w_out", [DFF, D])
    w_inm = dt_in("w_in_mix", [D, DMIX_IN])
    w_outm = dt_in("w_out_mix", [D, D])
    gcols_d = dt_in("gcols", [128, 40])
    fgain_d = dt_in("final_norm", [1, D])
    lbl_d = dt_in("lbl", [128, 16])
    qkrow_d = dt_in("qkrow", [1, 256])
    cos_d = dt_in("ropecos", [128, T])
    sin_d = dt_in("ropesin", [128, T])
    rot_d = dt_in("rotm", [128, 128])
    ident_d = dt_in("ident", [128, 128])
    mask_d = dt_in("trimask", [64, 128])
    chain_d = dt_in("chain", [128, 40])
    out_d = nc.dram_tensor("out", [T, D], F32, kind="ExternalOutput").ap()
    dbg = {}
    if debug:
        for n in ("dbg_x1", "dbg_x2"):
            dbg[n] = nc.dram_tensor(n, [T, D], F32, kind="ExternalOutput").ap()
        dbg["dbg_mt"] = nc.dram_tensor("dbg_mt", [128, 8 * T], F32, kind="ExternalOutput").ap()
        dbg["dbg_s"] = nc.dram_tensor("dbg_s", [128, 16], F32, kind="ExternalOutput").ap()
        dbg["dbg_h"] = nc.dram_tensor("dbg_h", [128, 512], F32, kind="ExternalOutput").ap()
        dbg["dbg_a"] = nc.dram_tensor("dbg_a", [128, 512], F32, kind="ExternalOutput").ap()
    xpark = nc.dram_tensor("xpark", [128, NT * D], F32)
    kv_in = [nc.dram_tensor("kv_in%d" % i, [128, 1024], F32) for i in range(4)]
    kv_all = [nc.dram_tensor("kv_all%d" % i, [4 * 128, 1024], F32) for i in range(4)]
    st_in = nc.dram_tensor("st_in", [128, SW], F32)
    st_all = nc.dram_tensor("st_all", [4 * 128, SW], F32)

    with contextlib.ExitStack() as st:
        s = Sched(nc, st)
        AW = 53100
        arena = st.enter_context(nc.sbuf_tensor("arena", [128, AW], F32))
        PS = [st.enter_context(nc.psum_tensor("ps%d" % i, [128, 512], F32)) for i in range(7)]
        PSR = [s.res("ps%d" % i) for i in range(7)]
        PTB = st.enter_context(nc.psum_tensor("psT", [128, 1024], BF16))
        PTBR = s.res("psT")

        class Alloc:
            def __init__(self):
                self.p = 0

            def f32(self, n, parts=128):
                o = self.p
                self.p += n
                assert self.p <= AW, ("arena overflow", self.p)
                return arena[0:parts, o:o + n]

            def bf(self, n, parts=128):
                w = (n + 1) // 2
                o = self.p
                self.p += w
                assert self.p <= AW, ("arena overflow", self.p)
                return arena[0:parts, o:o + w].bitcast(BF16)

            def bf2(self, n, parts=128):
                w = (n + 1) // 2
                o = self.p
                self.p += w
                assert self.p <= AW, ("arena overflow", self.p)
                return arena[0:parts, o:o + w].bitcast(BF16), arena[0:parts, o:o + w]

        A = Alloc()

        def mm(out, lhsT, rhs, start, stop, rd, wr, inc=True):
            s.op("pe", lambda e: e.matmul(out, lhsT=lhsT, rhs=rhs, start=start, stop=stop), rd, wr, inc=inc)

        def tr(out, in_, rd, wr, inc=True):
            s.op("pe", lambda e: e.transpose(out, in_, identb[0:in_.shape[0], 0:in_.shape[0]]), rd + [Rcp], wr, inc=inc)

        def act(out, in_, func, rd, wr, scale=1.0, bias=0.0, accum=None):
            if accum is None:
                s.op("act", lambda e: e.activation(out=out, in_=in_, func=func, bias=bias, scale=scale), rd, wr)
            else:
                s.op("act", lambda e: e.activation(out=out, in_=in_, func=func, bias=bias, scale=scale, accum_out=accum), rd, wr)

        def tt(eng, out, in0, in1, op, rd, wr):
            s.op(eng, lambda e: e.tensor_tensor(out=out, in0=in0, in1=in1, op=op), rd, wr)

        def ts(eng, out, in0, s1, op0, rd, wr, s2=None, op1=None):
            if op1 is None:
                s.op(eng, lambda e: e.tensor_scalar(out=out, in0=in0, scalar1=s1, scalar2=None, op0=op0), rd, wr)
            else:
                s.op(eng, lambda e: e.tensor_scalar(out=out, in0=in0, scalar1=s1, scalar2=s2, op0=op0, op1=op1), rd, wr)

        def stt(eng, out, in0, scalar, in1, op0, op1, rd, wr):
            s.op(eng, lambda e: e.scalar_tensor_tensor(out=out, in0=in0, scalar=scalar, in1=in1, op0=op0, op1=op1), rd, wr)

        def cp(eng, out, in_, rd, wr):
            if eng == "act":
                s.op("act", lambda e: e.copy(out=out, in_=in_), rd, wr)
            else:
                s.op(eng, lambda e: e.tensor_copy(out=out, in_=in_), rd, wr)

        def recip(out, in_, rd, wr):
            s.op("dve", lambda e: e.reciprocal(out=out, in_=in_), rd, wr)

        def rstd_from_ss(buf, n, R):
            ts("dve", buf, buf, 1.0 / n, ALU.mult, [R], [R], s2=EPS, op1=ALU.add)
            s.op("act", lambda e: e.sqrt(out=buf, in_=buf), [R], [R])
            recip(buf, buf, [R], [R])

        X = A.f32(NT * D).rearrange("p (t d) -> p t d", d=D)
        XR = [s.res("x%d" % t) for t in range(NT)]
        gcols = A.f32(40)
        lbl = A.f32(16)
        chain = A.f32(40)
        rotm = A.f32(128)
        ones32 = A.f32(128)
        identb = A.bf(128)
        onesb = A.bf(128)
        maskb = A.bf(128, parts=64)
        small = A.f32(64)
        Rconst = s.res("const")
        Rsmall = s.res("small")
        s.dma("sp", gcols, gcols_d, writes=[Rconst])
        s.dma("sp", lbl, lbl_d, writes=[Rconst])
        s.dma("sp", chain, chain_d, writes=[Rconst])
        s.dma("sp", rotm, rot_d, writes=[Rconst])
        Rcp = s.res("constp")
        s.dma("pool", identb, ident_d, writes=[Rcp])
        s.dma("pool", maskb, mask_d, writes=[Rcp])
        s.op("dve", lambda e: e.memset(ones32, 1.0), [], [Rconst])
        s.op("dve", lambda e: e.memset(onesb, 1.0), [], [Rconst])
        xv = x_d.rearrange("(t p) d -> p t d", p=128)
        for t4 in range(4):
            Rl = s.res("xl%d" % t4)
            s.dma("sp", X[:, 4 * t4:4 * t4 + 4, :], xv[:, 4 * t4:4 * t4 + 4, :], writes=[Rl] + XR[4 * t4:4 * t4 + 4])
        pbase = A.p

        GC_F1, GC_MIX, GC_F2 = 0, 8, 16
        GC_AO, GC_Q, GC_K, GC_RO = 24, 28, 29, 30

        def norm_T(hT, RhT, gc0):
            p0 = A.p
            junk = A.bf(D)
            Rj = s.res("junk")
            xn = [A.bf(D), A.bf(D)]
            Rxn = [s.res("xn0"), s.res("xn1")]
            ss = small[:, 0:16]
            s.op("dve", lambda e: e.memset(ss, 0.0), [], [Rsmall])
            for t in range(NT):
                act(junk, X[:, t, :], AF.Square, [XR[t]], [Rj, Rsmall], accum=ss[:, t:t + 1])
            rstd_from_ss(ss, D, Rsmall)
            for t in range(NT):
                b = t % 2
                s.op("act", lambda e, b=b, t=t: e.mul(out=xn[b], in_=X[:, t, :], mul=ss[:, t:t + 1]), [XR[t], Rsmall], [Rxn[b]])
                for c in range(8):
                    tr(PTB[:, c * 128:(c + 1) * 128], xn[b][:, c * 128:(c + 1) * 128], [Rxn[b]], [PTBR], inc=(c == 7))
                g3 = gcols[:, gc0:gc0 + 8].unsqueeze(2).to_broadcast([128, 8, 128])
                tt("dve", hT[:, :, t * 128:(t + 1) * 128], PTB[:, :].rearrange("p (c k) -> p c k", k=128), g3,
                   ALU.mult, [PTBR, Rconst], [RhT[t // 4]])

        def ffn(w_in, w_out, gc0, tagn):
            p0 = A.p
            hT = A.bf(8 * T).rearrange("p (c t) -> p c t", t=T)
            RhT = [s.res("hT%d" % i) for i in range(4)]
            norm_T(hT, RhT, gc0)
            if debug and tagn == 1:
                dh = A.f32(512)
                Rdh = s.res("dh")
                cp("dve", dh, hT[:, 0, 0:512], [RhT[0]], [Rdh])
                s.dma("sp", dbg["dbg_h"], dh, reads=[Rdh], writes=[s.res("dbgh")])
                s.dma("sp", dbg["dbg_s"], small[:, 0:16], reads=[Rsmall], writes=[s.res("dbgs")])
            win = [A.bf(8 * 512).rearrange("p (k n) -> p k n", n=512) for _ in range(2)]
            Rwin = [s.res("win0"), s.res("win1")]
            wout = [A.bf(2 * D).rearrange("p (c n) -> p c n", n=D) for _ in range(2)]
            Rwout = [s.res("wout0"), s.res("wout1")]
            actT = [A.bf(2 * T).rearrange("p (c t) -> p c t", t=T) for _ in range(2)]
            RactT = [[s.res("actT%d_%d" % (b, q)) for q in range(NQ)] for b in range(2)]
            sg = [A.f32(512), A.f32(512)]
            Rsg = [s.res("sg0"), s.res("sg1")]
            w_in_v = w_in.rearrange("(k p) n -> p k n", p=128)
            w_out_v = w_out.rearrange("(c p) n -> p c n", p=128)
            it = 0
            for g in range(NG):
                b = g % 2
                s.dma("pool", win[b][:, :, 0:256], w_in_v[:, :, g * 256:(g + 1) * 256], writes=[Rwin[b]])
                s.dma("pool", win[b][:, :, 256:512], w_in_v[:, :, DFF + g * 256:DFF + (g + 1) * 256], writes=[Rwin[b]])
                s.dma("pool", wout[b][:, :, :], w_out_v[:, 2 * g:2 * g + 2, :], writes=[Rwout[b]])
                for q in range(NQ):
                    for c in range(2):
                        pg, pu = it % 2, 2 + it % 2
                        for k in range(8):
                            mm(PS[pg][:, :], win[b][:, k, c * 128:(c + 1) * 128], hT[:, k, q * 512:(q + 1) * 512],
                               k == 0, k == 7, [Rwin[b], RhT[q]], [PSR[pg]], inc=(k == 7))
                        for k in range(8):
                            mm(PS[pu][:, :], win[b][:, k, 256 + c * 128:256 + (c + 1) * 128], hT[:, k, q * 512:(q + 1) * 512],
                               k == 0, k == 7, [Rwin[b], RhT[q]], [PSR[pu]], inc=(k == 7))
                        sb = it % 2
                        act(sg[sb], PS[pg][:, :], AF.Silu, [PSR[pg]], [Rsg[sb]])
                        tt("dve", actT[b][:, c, q * 512:(q + 1) * 512], sg[sb], PS[pu][:, :], ALU.mult,
                           [Rsg[sb], PSR[pu]], [RactT[b][q]])
                        it += 1
                        if debug and tagn == 1 and g == 0 and q == 0 and c == 0:
                            da = A.f32(512)
                            Rda = s.res("da")
                            cp("dve", da, actT[b][:, 0, 0:512], [RactT[b][0]], [Rda])
                            s.dma("sp", dbg["dbg_a"], da, reads=[Rda], writes=[s.res("dbga")])
                for t in range(NT):
                    for h in range(2):
                        pd = 4 + (2 * t + h) % 2
                        for c in range(2):
                            mm(PS[pd][:, :], actT[b][:, c, t * 128:(t + 1) * 128], wout[b][:, c, h * 512:(h + 1) * 512],
                               c == 0, c == 1, [RactT[b][t // 4], Rwout[b]], [PSR[pd]], inc=(c == 1))
                        stt("dve", X[:, t, h * 512:(h + 1) * 512], PS[pd][:, :], 0.5, X[:, t, h * 512:(h + 1) * 512],
                            ALU.mult, ALU.add, [PSR[pd], XR[t]], [XR[t]])
            s.barrier()
            A.p = p0

        def dump_x(name):
            if debug:
                Rd = s.res(name)
                s.dma("sp", dbg[name].rearrange("(t p) d -> p t d", p=128), X[:, :, :], reads=XR, writes=[Rd])
                s.barrier()

        ffn(w_in1, w_out1, GC_F1, 1)
        dump_x("dbg_x1")

        if stop_after != "ffn1":
            mixer(nc, s, A, locals())

        if stop_after is None:
            ffn(w_in2, w_out2, GC_F2, 2)
        p0 = A.p
        fg = A.f32(D)
        Rfg = s.res("fg")
        s.dma("sp", fg, fgain_d.partition_broadcast(128)[:, 0, :], writes=[Rfg])
        junk = A.bf(D)
        Rj = s.res("junkf")
        ss = small[:, 0:16]
        s.op("dve", lambda e: e.memset(ss, 0.0), [], [Rsmall])
        for t in range(NT):
            act(junk, X[:, t, :], AF.Square, [XR[t]], [Rj, Rsmall], accum=ss[:, t:t + 1])
        rstd_from_ss(ss, D, Rsmall)
        ov = out_d.rearrange("(t p) d -> p t d", p=128)
        Rout = s.res("out")
        for t in range(NT):
            stt("dve", X[:, t, :], X[:, t, :], ss[:, t:t + 1], fg, ALU.mult, ALU.mult, [XR[t], Rsmall, Rfg], [XR[t]])
            if t % 4 == 3:
                s.dma("sp", ov[:, t - 3:t + 1, :], X[:, t - 3:t + 1, :], reads=XR[t - 3:t + 1], writes=[Rout])
        s.barrier()
        s.emit()
    return nc


def mixer(nc, s, A, L):
    X, XR, PS, PSR, PTB, PTBR = L["X"], L["XR"], L["PS"], L["PSR"], L["PTB"], L["PTBR"]
    gcols, lbl, chain, rotm, ones32, identb, onesb, maskb, small = (L[k] for k in (
        "gcols", "lbl", "chain", "rotm", "ones32", "identb", "onesb", "maskb", "small"))
    Rconst, Rsmall = L["Rconst"], L["Rsmall"]
    mm, tr, act, tt, ts, stt, cp, recip, rstd_from_ss = (L[k] for k in (
        "mm", "tr", "act", "tt", "ts", "stt", "cp", "recip", "rstd_from_ss"))
    norm_T = L["norm_T"]
    debug, dbg = L["debug"], L["dbg"]
    w_inm, w_outm, cos_d, sin_d, qkrow_d = L["w_inm"], L["w_outm"], L["cos_d"], L["sin_d"], L["qkrow_d"]
    xpark, kv_in, kv_all, st_in, st_all = L["xpark"], L["kv_in"], L["kv_all"], L["st_in"], L["st_all"]
    GC_MIX, GC_AO, GC_Q, GC_K, GC_RO = L["GC_MIX"], L["GC_AO"], L["GC_Q"], L["GC_K"], L["GC_RO"]
    res = s.res
    p_mix = A.p
    hT = A.bf(8 * T).rearrange("p (c t) -> p c t", t=T)
    RhT = [res("mhT%d" % i) for i in range(4)]
    p_after_hT = A.p
    Rpark = res("xpark")
    xpv = xpark.ap().rearrange("p (t d) -> p t d", d=D)
    s.dma("sp", xpv, X[:, :, :], reads=XR, writes=[Rpark])
    norm_T(hT, RhT, GC_MIX)
    s.barrier()
    A.p = p_after_hT
    save_p = A.p
    A.p = 0
    QT = A.bf(4 * T).rearrange("p (h t) -> p h t", t=T)
    RQT = [res("QT%d" % h) for h in range(4)]
    MT = A.bf(8 * T).rearrange("p (c t) -> p c t", t=T)
    RMT = [res("MT%d" % i) for i in range(8)]
    STL = A.f32(SW)
    RSTL = res("STL")
    negB = A.f32(1)
    lbc = A.f32(16)
    Rlb = res("lb")
    smask = A.bf(T)
    Rsm = res("smask")
    assert A.p <= NT * D
    A.p = save_p
    p_small = A.p

    s.op("dve", lambda e: e.memset(smask, 1.0), [], [Rsm])
    s.op("dve", lambda e: e.memset(smask.rearrange("p (j c) -> p j c", c=C)[:, :, 0:1], 0.0), [], [Rsm])
    for d_ in range(2):
        tt("dve", lbc[:, d_ * 4:d_ * 4 + 4], lbl[:, d_ * 8:d_ * 8 + 4], lbl[:, d_ * 8 + 4:d_ * 8 + 8], ALU.subtract, [Rconst], [Rlb])
    act(lbc[:, 8:16], lbc[:, 0:8], AF.Sigmoid, [Rlb], [Rlb], scale=-1.0)

    qkrow = A.f32(256, parts=1)
    Rqk = res("qkrow")
    s.dma("sp", qkrow, qkrow_d, writes=[Rqk])
    mx = A.f32(4, parts=1)
    s.op("dve", lambda e: e.reduce_max(out=mx[:, 0:1], in_=qkrow[:, 0:128], axis=AX.X, apply_absolute_value=True), [Rqk], [Rqk])
    s.op("dve", lambda e: e.reduce_max(out=mx[:, 1:2], in_=qkrow[:, 128:256], axis=AX.X, apply_absolute_value=True), [Rqk], [Rqk])
    stt("dve", mx[:, 2:3], mx[:, 0:1], -float(np.sqrt(128.0)), mx[:, 1:2], ALU.mult, ALU.mult, [Rqk], [Rqk])
    mm(PS[6][:, 0:1], ones32[0:1, :], mx[:, 2:3], True, True, [Rqk, Rconst], [PSR[6]])
    cp("dve", negB, PS[6][:, 0:1], [PSR[6]], [Rlb])
    p_small = A.p

    w_v = w_inm.rearrange("(k p) n -> p k n", p=128)

    wq = [A.bf(8 * 256).rearrange("p (k n) -> p k n", n=256) for _ in range(2)]
    Rwq = [res("wq0"), res("wq1")]
    cosT = A.f32(T)
    sinT = A.f32(T)
    Rrope = res("rope")
    s.dma("sp", cosT, cos_d, writes=[Rrope])
    s.dma("sp", sinT, sin_d, writes=[Rrope])
    qfb = [A.f32(T), A.f32(T)]
    sqb = [A.f32(T), A.f32(T)]
    t1b = [A.f32(T), A.f32(T)]
    Rqfb = [[res("qf%d_%d" % (b, q)) for q in range(NQ)] for b in range(2)]
    Rsqb = [[res("sq%d_%d" % (b, q)) for q in range(NQ)] for b in range(2)]
    Rt1b = [[res("t1%d_%d" % (b, q)) for q in range(NQ)] for b in range(2)]
    KTo_b, KTo_w = A.bf2(2 * T)
    KTo = KTo_b.rearrange("p (h t) -> p h t", t=T)
    RKTo = res("KTo")
    Vo_b, Vo_w = A.bf2(NT * 256)
    Vo = Vo_b.rearrange("p (t n) -> p t n", n=256)
    RVo = res("Vo")
    QS = [slice(q * 512, (q + 1) * 512) for q in range(NQ)]
    pi = 0
    for pair in range(3):
        b = pair % 2
        s.dma("pool", wq[b][:, :, :], w_v[:, :, pair * 256:(pair + 1) * 256], writes=[Rwq[b]])
        for c in range(2):
            ch = pair * 2 + c
            bs = ch % 2
            qf, sq, t1 = qfb[bs], sqb[bs], t1b[bs]
            Rqf, Rsq, Rt1 = Rqfb[bs], Rsqb[bs], Rt1b[bs]
            gcol = gcols[:, GC_Q:GC_Q + 1] if ch < 4 else gcols[:, GC_K:GC_K + 1]
            for q, sl in enumerate(QS):
                pp = pi % 2
                pi += 1
                for k in range(8):
                    mm(PS[pp][:, :], wq[b][:, k, c * 128:(c + 1) * 128], hT[:, k, sl],
                       k == 0, k == 7, [Rwq[b], RhT[q]], [PSR[pp]], inc=(k == 7))
                cp("act", qf[:, sl], PS[pp][:, :], [PSR[pp]], [Rqf[q]])
                act(t1[:, sl], PS[pp][:, :], AF.Square, [PSR[pp]], [Rt1[q]])
                po1 = 2 + 2 * (q % 2)
                mm(PS[po1][:, :], ones32, t1[:, sl], True, True, [Rt1[q], Rconst], [PSR[po1]])
                cp("dve", sq[:, sl], PS[po1][:, :], [PSR[po1]], [Rsq[q]])
            for q, sl in enumerate(QS):
                rstd_from_ss(sq[:, sl], 128, Rsq[q])
            for q, sl in enumerate(QS):
                stt("dve", qf[:, sl], qf[:, sl], gcol, sq[:, sl], ALU.mult, ALU.mult, [Rqf[q], Rsq[q], Rconst], [Rqf[q]])
            for q, sl in enumerate(QS):
                po2 = 3 + 2 * (q % 2)
                mm(PS[po2][:, :], rotm, qf[:, sl], True, True, [Rqf[q], Rconst], [PSR[po2]])
                tt("dve", t1[:, sl], PS[po2][:, :], sinT[:, sl], ALU.mult, [PSR[po2], Rrope], [Rt1[q]])
            for q, sl in enumerate(QS):
                tt("dve", sq[:, sl], qf[:, sl], cosT[:, sl], ALU.mult, [Rqf[q], Rrope], [Rsq[q]])
            for q, sl in enumerate(QS):
                if ch < 4:
                    tt("dve", QT[:, ch, sl], sq[:, sl], t1[:, sl], ALU.add, [Rsq[q], Rt1[q]], [RQT[ch]])
                else:
                    tt("dve", KTo[:, ch - 4, sl], sq[:, sl], t1[:, sl], ALU.add, [Rsq[q], Rt1[q]], [RKTo])
    s.dma("pool", wq[1][:, :, :], w_v[:, :, 768:1024], writes=[Rwq[1]])
    for t in range(NT):
        pp = t % 2
        for k in range(8):
            mm(PS[pp][:, 0:256], hT[:, k, t * 128:(t + 1) * 128], wq[1][:, k, :], k == 0, k == 7,
               [Rwq[1], RhT[t // 4]], [PSR[pp]], inc=(k == 7))
        cp("act", Vo[:, t, :], PS[pp][:, 0:256], [PSR[pp]], [RVo])
    Rkvin = [res("kvin%d" % i) for i in range(4)]
    Rkvall = [res("kvall%d" % i) for i in range(4)]
    for i in range(4):
        srcw = (KTo_w if i < 2 else Vo_w)[:, (i % 2) * 1024:(i % 2 + 1) * 1024]
        s.dma("sp", kv_in[i].ap(), srcw, reads=[RKTo if i < 2 else RVo], writes=[Rkvin[i]])
        s.custom("pool", lambda e, i=i: e.collective_compute("AllGather", ALU.bypass, replica_groups=[[0, 1, 2, 3], [4, 5, 6, 7]],
                                                             ins=[kv_in[i].ap().opt()], outs=[kv_all[i].ap().opt()]),
                 Rkvall[i], 1, reads=[Rkvin[i]], writes=[Rkvall[i]])
    s.barrier()
    A.p = p_small

    def load_wr(h, wr, Rwr):
        for i in range(5):
            s.dma("pool", wr[:, :, i * 128:(i + 1) * 128], w_v[:, :, 1024 + 512 * i + h * 128:1024 + 512 * i + (h + 1) * 128],
                  writes=[Rwr])

    def hgrn_head(h, full, wrs, SA=None, coef=None, RSA=None, Rcoef=None):
        p0 = A.p
        wr, Rwr = wrs[h % 2]
        kf, gf, bb = A.f32(T), A.f32(T), A.f32(T)
        Rkf = [res("kf%d" % q) for q in range(NQ)]
        Rgf = [res("gf%d" % q) for q in range(NQ)]
        Rbb = [res("bb%d" % q) for q in range(NQ)]
        Lq = A.f32(4)
        RLq = res("Lq")
        totc = A.f32(NCH)
        Rtot = res("totc")
        EL = A.f32(2 * NCH).rearrange("p (a j) -> p a j", j=NCH)
        REL = res("EL")
        KTl = [A.bf(T) for _ in range(2)]
        RKTl = [res("KTl0"), res("KTl1")]
        KHb = A.bf(T)
        RKHb = [res("KHb%d" % q) for q in range(NQ)]
        KH = [A.bf(NCH * 128, parts=64).rearrange("p (j d) -> p j d", d=128) for _ in range(2)]
        RKH = [res("KH0"), res("KH1")]
        Vt = A.bf(NCH * 128, parts=64).rearrange("p (j d) -> p j d", d=128)
        RVt = res("Vt")
        Sst = [A.f32(128), A.f32(128)]
        Sbf = [A.bf(128), A.bf(128)]
        RS = [res("S0"), res("S1")]
        RSb = [res("Sb0"), res("Sb1")]
        if full:
            qh = A.f32(T)
            Rqh = res("qh")
            QF = [A.bf(T), A.bf(T)]
            RQF = [res("QF0"), res("QF1")]
            GTh = A.bf(T)
            RGT = res("GTh")
            oT = kf
            RoTq = Rkf
            Am = [[A.bf(64, parts=64), A.bf(64, parts=64)] for _ in range(2)]
            RAm = [[res("Am%d_%d" % (d_, i)) for i in range(2)] for d_ in range(2)]
            Sbf2 = [[A.bf(128), A.bf(128)] for _ in range(2)]
            RSb2 = [[res("Sb%d_%d" % (d_, i)) for i in range(2)] for d_ in range(2)]
            SAh = A.f32(4 * 256).rearrange("p (r w) -> p r w", w=256)
            RSAh = res("SAh")
            s.dma("sp", SAh, st_all.ap().rearrange("(r p) w -> p r w", p=128)[:, :, h * 256:(h + 1) * 256],
                  reads=[RSA], writes=[RSAh])
        if h < 3:
            load_wr(h + 1, *wrs[(h + 1) % 2])
        pi = 0
        if full:
            for q in range(NQ):
                sl = slice(q * 512, (q + 1) * 512)
                pp = pi % 2
                pi += 1
                for k in range(8):
                    mm(PS[pp][:, :], wr[:, k, 0:128], hT[:, k, sl], k == 0, k == 7, [Rwr, RhT[q]], [PSR[pp]], inc=(k == 7))
                cp("dve", qh[:, sl], PS[pp][:, :], [PSR[pp]], [Rqh])
                pp = pi % 2
                pi += 1
                for k in range(8):
                    mm(PS[pp][:, :], wr[:, k, 512:640], hT[:, k, sl], k == 0, k == 7, [Rwr, RhT[q]], [PSR[pp]], inc=(k == 7))
                act(GTh[:, sl], PS[pp][:, :], AF.Silu, [PSR[pp]], [RGT])
        for j in range(NCH):
            pp = j % 2
            for k in range(8):
                mm(PS[pp][0:64, 0:128], hT[:, k, j * C:(j + 1) * C], wr[:, k, 384:512], k == 0, k == 7,
                   [Rwr, RhT[j // 8]], [PSR[pp]], inc=(k == 7))
            cp("act", Vt[:, j, :], PS[pp][0:64, 0:128], [PSR[pp]], [RVt])
        QS = [slice(q * 512, (q + 1) * 512) for q in range(NQ)]
        for d_ in range(2):
            a = h * 2 + d_
            oml = lbc[:, 8 + d_ * 4 + h:8 + d_ * 4 + h + 1]
            lastc = (C - 1) if d_ == 0 else 0
            for q, sl in enumerate(QS):
                pp = pi % 2
                pi += 1
                for k in range(8):
                    mm(PS[pp][:, :], wr[:, k, 128 + d_ * 128:256 + d_ * 128], hT[:, k, sl], k == 0, k == 7,
                       [Rwr, RhT[q]], [PSR[pp]], inc=(k == 7))
                act(kf[:, sl], PS[pp][:, :], AF.Sigmoid, [PSR[pp]], [Rkf[q]], scale=-1.0)
            for q, sl in enumerate(QS):
                ts("dve", kf[:, sl], kf[:, sl], oml, ALU.mult, [Rkf[q], Rlb], [Rkf[q]])
            for q, sl in enumerate(QS):
                act(gf[:, sl], kf[:, sl], AF.Ln, [Rkf[q]], [Rgf[q]], scale=-1.0, bias=1.0)
            for q, sl in enumerate(QS):
                if not full:
                    s.op("dve", lambda e, q=q, sl=sl: e.tensor_reduce(out=Lq[:, q:q + 1], in_=gf[:, sl], axis=AX.X, op=ALU.add),
                         [Rgf[q]], [RLq])
                s.op("dve", lambda e, sl=sl: e.tensor_tensor_scan(out=bb[:, sl], data0=smask[:, sl], data1=gf[:, sl], initial=0.0,
                                                                  op0=ALU.mult, op1=ALU.add), [Rsm, Rgf[q]], [Rbb[q]])
                if d_ == 1:
                    bq3 = bb[:, sl].rearrange("p (j c) -> p j c", c=C)
                    tt("dve", gf[:, sl], bb[:, sl], gf[:, sl], ALU.subtract, [Rbb[q], Rgf[q]], [Rgf[q]])
                    cp("dve", totc[:, q * 8:(q + 1) * 8], bq3[:, :, C - 1], [Rbb[q]], [Rtot])
                    tt("dve", bq3, totc[:, q * 8:(q + 1) * 8].unsqueeze(2).to_broadcast([128, 8, C]),
                       gf[:, sl].rearrange("p (j c) -> p j c", c=C), ALU.subtract, [Rgf[q], Rtot], [Rbb[q]])
            if not full:
                s.op("dve", lambda e, a=a: e.tensor_reduce(out=STL[:, 1024 + a:1025 + a], in_=Lq, axis=AX.X, op=ALU.add),
                     [RLq], [RSTL])
            for q, sl in enumerate(QS):
                act(gf[:, sl], bb[:, sl], AF.Exp, [Rbb[q]], [Rgf[q]])
            for q, sl in enumerate(QS):
                cp("dve", EL[:, d_, q * 8:(q + 1) * 8], gf[:, sl].rearrange("p (j c) -> p j c", c=C)[:, :, lastc], [Rgf[q]], [REL])
                if full:
                    tt("dve", QF[d_][:, sl], qh[:, sl], gf[:, sl], ALU.mult, [Rqh, Rgf[q]], [RQF[d_]])
            for q, sl in enumerate(QS):
                act(gf[:, sl], bb[:, sl], AF.Exp, [Rbb[q]], [Rgf[q]], scale=-1.0)
            for q, sl in enumerate(QS):
                tt("dve", KTl[d_][:, sl], kf[:, sl], gf[:, sl], ALU.mult, [Rkf[q], Rgf[q]], [RKTl[d_]])
                tt("dve", KHb[:, sl].rearrange("p (j c) -> p j c", c=C), KTl[d_][:, sl].rearrange("p (j c) -> p j c", c=C),
                   EL[:, d_, q * 8:(q + 1) * 8].unsqueeze(2).to_broadcast([128, 8, C]), ALU.mult, [RKTl[d_], REL], [RKHb[q]])
                for j in range(q * 8, q * 8 + 8):
                    tr(PTB[0:64, (j % 8) * 128:(j % 8 + 1) * 128], KHb[:, j * C:(j + 1) * C], [RKHb[q]], [PTBR], inc=(j % 8 == 7))
                cp("act", KH[d_][:, q * 8:q * 8 + 8, :], PTB[0:64, :].rearrange("p (j d) -> p j d", d=128), [PTBR], [RKH[d_]])
        if full:
            s.op("dve", lambda e: e.memset(oT, 0.0), [], RoTq)
        for d_ in range(2):
            a = h * 2 + d_
            if full:
                for p in range(4):
                    Sp = SAh[:, p, d_ * 128:(d_ + 1) * 128]
                    if p == 0:
                        ts("dve", Sst[d_], Sp, coef[:, p, a:a + 1], ALU.mult, [RSAh, Rcoef], [RS[d_]])
                    else:
                        stt("dve", Sst[d_], Sp, coef[:, p, a:a + 1], Sst[d_], ALU.mult, ALU.add, [RSAh, Rcoef, RS[d_]], [RS[d_]])
                cp("act", Sbf2[d_][0], Sst[d_], [RS[d_]], [RSb2[d_][0]])
        def jof(step, d_):
            return step if d_ == 0 else NCH - 1 - step

        def stageA(step, d_):
            j = jof(step, d_)
            cs = slice(j * C, (j + 1) * C)
            pa = 2 + d_
            ca = (step % 2) * 64
            mm(PS[pa][0:64, ca:ca + 64], KTl[d_][:, cs], QF[d_][:, cs], True, True, [RKTl[d_], RQF[d_]], [PSR[pa]])
            tt("dve", Am[d_][step % 2], PS[pa][0:64, ca:ca + 64], maskb[:, d_ * 64:(d_ + 1) * 64], ALU.mult,
               [PSR[pa], L["Rcp"]], [RAm[d_][step % 2]])

        def stageKV(step, d_):
            a = h * 2 + d_
            j = jof(step, d_)
            pk = 6
            mm(PS[pk][:, (d_ * 128):(d_ * 128 + 128)], KH[d_][:, j, :], Vt[:, j, :], True, True, [RKH[d_], RVt], [PSR[pk]])
            if step == 0 and not full:
                cp("dve", Sst[d_], PS[pk][:, d_ * 128:d_ * 128 + 128], [PSR[pk]], [RS[d_]])
            else:
                stt("dve", Sst[d_], Sst[d_], EL[:, d_, j:j + 1], PS[pk][:, d_ * 128:d_ * 128 + 128], ALU.mult, ALU.add,
                    [RS[d_], REL, PSR[pk]], [RS[d_]])
            if step < NCH - 1:
                if full:
                    cp("act", Sbf2[d_][(step + 1) % 2], Sst[d_], [RS[d_]], [RSb2[d_][(step + 1) % 2]])
            elif not full:
                cp("act", STL[:, a * 128:(a + 1) * 128], Sst[d_], [RS[d_]], [RSTL])

        def stageO(step, d_):
            j = jof(step, d_)
            cs = slice(j * C, (j + 1) * C)
            po = 4 + d_
            oc = (step % 8) * C
            mm(PS[po][:, oc:oc + C], Vt[:, j, :], Am[d_][step % 2], True, False, [RVt, RAm[d_][step % 2]], [PSR[po]], inc=False)
            mm(PS[po][:, oc:oc + C], Sbf2[d_][step % 2], QF[d_][:, cs], False, True, [RSb2[d_][step % 2], RQF[d_]], [PSR[po]])
            if step % 8 == 7:
                if d_ == 0:
                    j0 = step - 7
                    tt("dve", oT[:, j0 * C:(j0 + 8) * C], oT[:, j0 * C:(j0 + 8) * C], PS[po][:, :], ALU.add,
                       [PSR[po], RoTq[j0 // 8]], [RoTq[j0 // 8]])
                else:
                    src = PS[po][:, :].rearrange("p (s c) -> p s c", c=C)
                    for s8 in range(8):
                        jj = j + 7 - s8
                        tt("dve", oT[:, jj * C:(jj + 1) * C], oT[:, jj * C:(jj + 1) * C], src[:, s8, :], ALU.add,
                           [PSR[po], RoTq[jj // 8]], [RoTq[jj // 8]])

        if full:
            for d_ in range(2):
                stageA(0, d_)
        for step in range(NCH):
            for d_ in range(2):
                stageKV(step, d_)
                if full:
                    if step + 1 < NCH:
                        stageA(step + 1, d_)
                    stageO(step, d_)
        if full:
            sqo = gf
            rso = bb
            for q in range(NQ):
                sl = slice(q * 512, (q + 1) * 512)
                act(sqo[:, sl], oT[:, sl], AF.Square, [RoTq[q]], [Rgf[q]])
                mm(PS[0][:, :], ones32, sqo[:, sl], True, True, [Rgf[q], Rconst], [PSR[0]])
                cp("dve", rso[:, sl], PS[0][:, :], [PSR[0]], [Rbb[q]])
                rstd_from_ss(rso[:, sl], 128, Rbb[q])
                stt("dve", oT[:, sl], oT[:, sl], gcols[:, GC_RO:GC_RO + 1], rso[:, sl], ALU.mult, ALU.mult,
                    [RoTq[q], Rbb[q], Rconst], [RoTq[q]])
                tt("dve", MT[:, 4 + h, sl], oT[:, sl], GTh[:, sl], ALU.mult, [RoTq[q], RGT], [RMT[4 + h]])
        s.barrier()
        A.p = p0

    s.op("dve", lambda e: e.memset(STL, 0.0), [], [RSTL])
    wrs = [(A.bf(8 * 640).rearrange("p (k n) -> p k n", n=640), res("wr%d" % i)) for i in range(2)]
    load_wr(0, *wrs[0])
    for h in range(4):
        hgrn_head(h, False, wrs)
    Rstin = res("stin")
    Rstall = res("stall")
    s.dma("sp", st_in.ap(), STL, reads=[RSTL], writes=[Rstin])
    s.custom("pool", lambda e: e.collective_compute("AllGather", ALU.bypass, replica_groups=[[0, 1, 2, 3], [4, 5, 6, 7]],
                                                    ins=[st_in.ap().opt()], outs=[st_all.ap().opt()]),
             Rstall, 1, reads=[Rstin], writes=[Rstall])
    s.barrier()
    A.p = p_small

    KT_b, KT_w = A.bf2(2 * 8192)
    KT = KT_b.rearrange("p (h t) -> p h t", t=8192)
    VA_b, VA_w = A.bf2(64 * 256)
    VA = VA_b.rearrange("p (c n) -> p c n", n=256)
    RKT, RVA = res("KT"), res("VA")
    for p in range(4):
        for g in range(2):
            s.dma("sp", KT_w[:, g * 4096 + p * 1024:g * 4096 + (p + 1) * 1024], kv_all[g].ap()[p * 128:(p + 1) * 128, :],
                  reads=[Rkvall[g]], writes=[RKT])
        for i in (2, 3):
            c0 = (p * 16 + (i - 2) * 8) * 128
            s.dma("sp", VA_w[:, c0:c0 + 1024], kv_all[i].ap()[p * 128:(p + 1) * 128, :], reads=[Rkvall[i]], writes=[RVA])
    PTs = [A.bf(512) for _ in range(3)]
    RPT = [res("PT%d" % i) for i in range(3)]
    AT = A.f32(4 * 512).rearrange("p (h t) -> p h t", t=512)
    RAT = res("AT")
    rl = A.f32(512)
    Rrl = res("rl")
    accd = [A.f32(512), A.f32(512)]
    Raccd = [res("accd0"), res("accd1")]
    sqa = A.f32(512)
    Rsqa = res("sqa")
    rsa = A.f32(512)
    Rrsa = res("rsa")
    scale = float(128.0 ** -0.5)
    iters = [(q, h, kc) for q in range(NQ) for h in range(4) for kc in range(64)]
    SB = [0, 1, 6]
    LA = 2

    def issue_S(i):
        q, h, kc = iters[i]
        g = h // 2
        pb = SB[i % 3]
        pt = i % 3
        mm(PS[pb][:, :], KT[:, g, kc * 128:(kc + 1) * 128], QT[:, h, q * 512:(q + 1) * 512], True, True,
           [RKT, RQT[h]], [PSR[pb]])
        act(PTs[pt], PS[pb][:, :], AF.Exp, [PSR[pb], Rlb], [RPT[pt]], scale=scale, bias=negB)

    def issue_PV(i):
        q, h, kc = iters[i]
        g = h // 2
        sl = slice(q * 512, (q + 1) * 512)
        pt = i % 3
        po, pl = 2 + (q * 4 + h) % 2, 4 + (q * 4 + h) % 2
        mm(PS[po][:, :], VA[:, kc, g * 128:(g + 1) * 128], PTs[pt], kc == 0, kc == 63, [RVA, RPT[pt]], [PSR[po]])
        ab = (q * 4 + h) % 2
        if kc == 0:
            cp("dve", accd[ab], PTs[pt], [RPT[pt]], [Raccd[ab]])
        else:
            tt("dve", accd[ab], accd[ab], PTs[pt], ALU.add, [Raccd[ab], RPT[pt]], [Raccd[ab]])
        if kc == 63:
            mm(PS[pl][:, :], ones32, accd[ab], True, True, [Raccd[ab], Rconst], [PSR[pl]])
            recip(rl, PS[pl][:, :], [PSR[pl]], [Rrl])
            tt("dve", AT[:, h, :], PS[po][:, :], rl, ALU.mult, [PSR[po], Rrl], [RAT])
            if h == 3:
                for hh in range(4):
                    act(sqa, AT[:, hh, :], AF.Square, [RAT], [Rsqa])
                    mm(PS[3][:, :], ones32, sqa, hh == 0, hh == 3, [Rsqa, Rconst], [PSR[3]], inc=True)
                cp("dve", rsa, PS[3][:, :], [PSR[3]], [Rrsa])
                rstd_from_ss(rsa, 512, Rrsa)
                for hh in range(4):
                    stt("dve", MT[:, hh, sl], AT[:, hh, :], gcols[:, GC_AO + hh:GC_AO + hh + 1], rsa, ALU.mult, ALU.mult,
                        [RAT, Rrsa, Rconst], [RMT[hh]])

    for i in range(min(LA, len(iters))):
        issue_S(i)
    for i in range(len(iters)):
        if i + LA < len(iters):
            issue_S(i + LA)
        issue_PV(i)
    s.barrier()
    A.p = p_small

    SA = A.f32(4 * 8).rearrange("p (r w) -> p r w", w=8)
    RSA = Rstall
    RSAL = res("SAL")
    s.dma("sp", SA, st_all.ap().rearrange("(r p) w -> p r w", p=128)[:, :, 1024:1032], reads=[Rstall], writes=[RSAL])
    cl = A.f32(8)
    Rcl = res("cl")
    coef = A.f32(4 * 8).rearrange("p (r a) -> p r a", a=8)
    Rcoef = res("coef")
    for d_ in range(2):
        for p in range(4):
            base = d_ * 20 + p * 5
            for j in range(4):
                Lj = SA[:, j, d_:8:2]
                wcol = chain[:, base + j:base + j + 1]
                if j == 0:
                    ts("dve", cl[:, 0:4], Lj, wcol, ALU.mult, [RSAL, Rconst], [Rcl])
                else:
                    stt("dve", cl[:, 0:4], Lj, wcol, cl[:, 0:4], ALU.mult, ALU.add, [RSAL, Rconst, Rcl], [Rcl])
            act(cl[:, 4:8], cl[:, 0:4], AF.Exp, [Rcl], [Rcl])
            ts("dve", coef[:, p, d_:8:2], cl[:, 4:8], chain[:, base + 4:base + 5], ALU.mult, [Rcl, Rconst], [Rcoef])
    wrs = [(A.bf(8 * 640).rearrange("p (k n) -> p k n", n=640), res("wr%d" % i)) for i in range(2)]
    load_wr(0, *wrs[0])
    for h in range(4):
        hgrn_head(h, True, wrs, SA=SA, coef=coef, RSA=RSA, Rcoef=Rcoef)
    A.p = p_small
    if debug:
        dtmp = A.f32(T)
        Rdt = res("dtmp")
        Rdm = res("dbgmt")
        for c8 in range(8):
            cp("dve", dtmp, MT[:, c8, :], [RMT[c8]], [Rdt])
            s.dma("sp", dbg["dbg_mt"][:, c8 * T:(c8 + 1) * T], dtmp, reads=[Rdt], writes=[Rdm])
        s.barrier()
        A.p = p_small
    A.p = p_mix
    X2 = A.f32(NT * D).rearrange("p (t d) -> p t d", d=D)
    RX2 = [res("x2_%d" % t) for t in range(NT)]
    for t4 in range(4):
        s.dma("sp", X2[:, 4 * t4:4 * t4 + 4, :], xpv[:, 4 * t4:4 * t4 + 4, :], reads=[Rpark], writes=RX2[4 * t4:4 * t4 + 4])
    wo = A.bf(8 * D).rearrange("p (c n) -> p c n", n=D)
    Rwo = res("wo")
    s.dma("pool", wo, w_outm.rearrange("(c p) n -> p c n", p=128), writes=[Rwo])
    for t in range(NT):
        for hh in range(2):
            pd = (2 * t + hh) % 2
            for c8 in range(8):
                mm(PS[pd][:, :], MT[:, c8, t * 128:(t + 1) * 128], wo[:, c8, hh * 512:(hh + 1) * 512], c8 == 0, c8 == 7,
                   [RMT[c8], Rwo], [PSR[pd]], inc=(c8 == 7))
            tt("dve", X2[:, t, hh * 512:(hh + 1) * 512], X2[:, t, hh * 512:(hh + 1) * 512], PS[pd][:, :], ALU.add,
               [RX2[t], PSR[pd]], [RX2[t]])
    s.barrier()
    for t in range(NT):
        cp(("act", "dve")[t % 2], X[:, t, :], X2[:, t, :], [RX2[t]], [XR[t]])
    s.barrier()
    A.p = p_mix
    L["dump_x"]("dbg_x2")


_NC_CACHE = {}


def _consts(core):
    r = core % 4
    tok0 = r * T
    pos = np.arange(tok0, tok0 + T)
    rows = (pos // 64).astype(np.float32)
    cols = (pos % 64).astype(np.float32)
    inv = (np.float32(10000.0) ** (-np.arange(0, 64, 2, dtype=np.float32) / np.float32(64))).astype(np.float32)
    ang_r = (rows[:, None] * inv[None, :]).astype(np.float32)
    ang_c = (cols[:, None] * inv[None, :]).astype(np.float32)
    cr, sr, cc, sc = np.cos(ang_r), np.sin(ang_r), np.cos(ang_c), np.sin(ang_c)
    cosT = np.concatenate([cr, cr, cc, cc], axis=1).T.astype(np.float32)
    sinT = np.concatenate([sr, sr, sc, sc], axis=1).T.astype(np.float32)
    rot = np.zeros((128, 128), np.float32)
    for i in range(32):
        rot[32 + i, i] = -1.0
        rot[i, 32 + i] = 1.0
        rot[96 + i, 64 + i] = -1.0
        rot[64 + i, 96 + i] = 1.0
    mask = np.zeros((64, 128), np.float32)
    sI, tI = np.meshgrid(np.arange(64), np.arange(64), indexing="ij")
    mask[:, 0:64] = (sI <= tI)
    mask[:, 64:128] = (sI >= tI)
    ch = np.zeros((2, 4, 5), np.float32)
    for p in range(4):
        if p < r:
            ch[0, p, 4] = 1.0
            for j in range(4):
                ch[0, p, j] = 1.0 if (p < j < r) else 0.0
        if p > r:
            ch[1, p, 4] = 1.0
            for j in range(4):
                ch[1, p, j] = 1.0 if (r < j < p) else 0.0
    chain = np.broadcast_to(ch.reshape(1, 40), (128, 40)).astype(np.float32)
    return cosT, sinT, rot, mask, np.ascontiguousarray(chain)


def make_in_maps(inputs):
    f = lambda k: np.ascontiguousarray(np.asarray(inputs[k], dtype=np.float32))
    x = f("x")
    gc = np.zeros((128, 40), np.float32)
    gc[:, 0:8] = f("ffn1_norm")[0].reshape(8, 128).T
    gc[:, 8:16] = f("mix_norm")[0].reshape(8, 128).T
    gc[:, 16:24] = f("ffn2_norm")[0].reshape(8, 128).T
    gc[:, 24:28] = f("attn_out_norm")[0].reshape(4, 128).T
    gc[:, 28] = f("attn_q_norm")[0]
    gc[:, 29] = f("attn_k_norm")[0]
    gc[:, 30] = f("rec_out_norm")[0]
    lb = f("rec_lb_logits")
    lbl = np.ascontiguousarray(lb.reshape(2, 2, 4, 128).transpose(3, 0, 1, 2).reshape(128, 16))
    qkrow = np.concatenate([f("attn_q_norm")[0], f("attn_k_norm")[0]])[None, :]
    shared = {
        "ffn1_w_in": f("ffn1_w_in")[0], "ffn1_w_out": f("ffn1_w_out")[0],
        "ffn2_w_in": f("ffn2_w_in")[0], "ffn2_w_out": f("ffn2_w_out")[0],
        "w_in_mix": f("w_in_mix")[0], "w_out_mix": f("w_out_mix")[0],
        "gcols": gc, "final_norm": f("final_norm"), "lbl": lbl, "qkrow": np.ascontiguousarray(qkrow),
        "ident": np.eye(128, dtype=np.float32),
    }
    maps = []
    for c in range(NCORES):
        b, r = c // 4, c % 4
        cosT, sinT, rot, mask, chain = _consts(c)
        m = dict(shared)
        m["x"] = np.ascontiguousarray(x[b, r * T:(r + 1) * T, :])
        m.update({"ropecos": cosT, "ropesin": sinT, "rotm": rot, "trimask": mask, "chain": chain})
        maps.append(m)
    return maps


def kernel(**inputs):
    debug = bool(int(os.environ.get("MK_DEBUG", "0")))
    stop_after = os.environ.get("MK_STOP") or None
    key = (debug, stop_after)
    if key not in _NC_CACHE:
        _NC_CACHE[key] = build(debug=debug, stop_after=stop_after)
    nc = _NC_CACHE[key]
    maps = make_in_maps(inputs)
    res = run_bass_kernel_spmd(nc, maps, core_ids=list(range(NCORES)))
    out = np.zeros((2, 8192, D), np.float32)
    for c in range(NCORES):
        b, r = c // 4, c % 4
        out[b, r * T:(r + 1) * T, :] = res.results[c]["out"]
    if debug:
        kernel.last_debug = res.results
    return out
```

```python
import contextlib
import os
import numpy as np
import concourse.bass as bass
import concourse.mybir as mybir
from concourse.bass_utils import run_bass_kernel_spmd

F32 = mybir.dt.float32
BF16 = mybir.dt.bfloat16
AF = mybir.ActivationFunctionType
ALU = mybir.AluOpType
AX = mybir.AxisListType

NCORES = 8
D = 1024
T = 2048
NT = 16
NQ = 4
DFF = 2816
NG = 11
DMIX_IN = 3584
EPS = 1e-6
C = 64
NCH = T // C
SW = 4 * 2 * 128 + 8


class Res:
    __slots__ = ("name", "w", "r", "dsem", "dcnt")

    def __init__(self, name):
        self.name = name
        self.w = None
        self.r = {}
        self.dsem = None
        self.dcnt = 0


class Sched:
    def __init__(self, nc, stack, ndummy=8):
        self.nc = nc
        self.stack = stack
        self.dummy = [stack.enter_context(nc.semaphore("dummy%d" % i)) for i in range(ndummy)]
        self.eng = {"pe": nc.tensor, "act": nc.scalar, "dve": nc.vector,
                    "pool": nc.gpsimd, "sp": nc.sync}
        self.ops = {e: [] for e in self.eng}
        self.sem = {e: stack.enter_context(nc.semaphore("sem_" + e)) for e in self.eng}
        self.cnt = {e: 0 for e in self.eng}
        self.pending = {e: False for e in self.eng}
        self.waited = {e: {} for e in self.eng}
        self.nsem = 0
        self.dres = []

    def res(self, name):
        return Res(name)

    def _need(self, e, ev):
        if ev is None:
            return
        sem, val = ev
        k = sem.num
        if k == self.sem[e].num and val > self.cnt[e]:
            return
        if self.waited[e].get(k, 0) >= val:
            return
        self.waited[e][k] = val
        self.ops[e].append(lambda eng, sem=sem, val=val: eng.wait_ge(sem, val))

    def _deps(self, e, reads, writes):
        for R in reads:
            self._need(e, R.w)
        for R in writes:
            self._need(e, R.w)
            for ev in list(R.r.values()):
                self._need(e, ev)

    def _mark(self, ev, reads, writes):
        sem, val = ev
        for R in reads:
            old = R.r.get(sem.num)
            if old is None or old[1] < val:
                R.r[sem.num] = ev
        for R in writes:
            R.w = ev
            R.r = {}

    def op(self, e, fn, reads=(), writes=(), inc=True):
        self._deps(e, reads, writes)
        sem = self.sem[e]
        if inc:
            self.cnt[e] += 1
            tick = self.cnt[e]
            self.pending[e] = False
            self.ops[e].append(lambda eng, fn=fn, sem=sem: fn(eng).then_inc(sem, 1))
        else:
            tick = self.cnt[e] + 1
            self.pending[e] = True
            self.ops[e].append(lambda eng, fn=fn: fn(eng))
        self._mark((sem, tick), reads, writes)

    def _dsem(self, R, pre):
        if R.dsem is None:
            R.dsem = self.stack.enter_context(self.nc.semaphore("%s%d_%s" % (pre, self.nsem, R.name)))
            self.nsem += 1
            self.dres.append(R)

    def dma(self, q, out, in_, reads=(), writes=(), track=None):
        R = track if track is not None else (writes[0] if writes else reads[0])
        self._dsem(R, "d")
        for Rr in reads:
            self._need(q, Rr.w)
        for Rw in writes:
            if not (Rw.w is not None and Rw.dsem is not None and Rw.w[0].num == Rw.dsem.num):
                self._need(q, Rw.w)
            for ev in list(Rw.r.values()):
                self._need(q, ev)
        R.dcnt += 16
        dsem = R.dsem
        self.ops[q].append(lambda eng, out=out, in_=in_, dsem=dsem:
                           eng.dma_start(out=out, in_=in_).then_inc(dsem, 16))
        ev = (dsem, R.dcnt)
        for Rr in reads:
            old = Rr.r.get(dsem.num)
            if old is None or old[1] < ev[1]:
                Rr.r[dsem.num] = ev
        for Rw in writes:
            Rw.w = ev
        return ev

    def custom(self, e, fn, R, incval, reads=(), writes=()):
        self._dsem(R, "c")
        self._deps(e, reads, writes)
        R.dcnt += incval
        dsem = R.dsem
        self.ops[e].append(lambda eng, fn=fn, dsem=dsem, incval=incval: fn(eng).then_inc(dsem, incval))
        ev = (dsem, R.dcnt)
        self._mark(ev, reads, writes)
        return ev

    def barrier(self):
        for e in self.eng:
            assert not self.pending[e]
        for e in self.eng:
            for e2 in self.eng:
                if self.cnt[e2] > 0:
                    self._need(e, (self.sem[e2], self.cnt[e2]))
            for R in self.dres:
                if R.dsem.name.startswith("c"):
                    continue
                self._need(e, (R.dsem, R.dcnt))

    def emit(self):
        nc = self.nc
        for e in self.eng:
            assert not self.pending[e], e
        ops = self.ops
        with nc.Block() as block:
            @block.tensor
            def _(eng):
                for f in ops["pe"]:
                    f(eng)

            @block.scalar
            def _(eng):
                for f in ops["act"]:
                    f(eng)

            @block.vector
            def _(eng):
                for f in ops["dve"]:
                    f(eng)

            @block.gpsimd
            def _(eng):
                for f in ops["pool"]:
                    f(eng)

            @block.sync
            def _(eng):
                for f in ops["sp"]:
                    f(eng)


def build(debug=False, stop_after=None):
    nc = bass.Bass("TRN2", target_bir_lowering=False)
    dt_in = lambda n, shp: nc.dram_tensor(n, shp, F32, kind="ExternalInput").ap()
    x_d = dt_in("x", [T, D])
    w_in1 = dt_in("ffn1_w_in", [D, 2 * DFF])
    w_out1 = dt_in("ffn1_w_out", [DFF, D])
    w_in2 = dt_in("ffn2_w_in", [D, 2 * DFF])
    w_out2 = dt_in("ffn2_w_out", [DFF, D])
    w_inm = dt_in("w_in_mix", [D, DMIX_IN])
    w_outm = dt_in("w_out_mix", [D, D])
    gcols_d = dt_in("gcols", [128, 40])
    fgain_d = dt_in("final_norm", [1, D])
    lbl_d = dt_in("lbl", [128, 16])
    qkrow_d = dt_in("qkrow", [1, 256])
    cos_d = dt_in("ropecos", [128, T])
    sin_d = dt_in("ropesin", [128, T])
    rot_d = dt_in("rotm", [128, 128])
    ident_d = dt_in("ident", [128, 128])
    mask_d = dt_in("trimask", [64, 128])
    chain_d = dt_in("chain", [128, 40])
    out_d = nc.dram_tensor("out", [T, D], F32, kind="ExternalOutput").ap()
    dbg = {}
    if debug:
        for n in ("dbg_x1", "dbg_x2"):
            dbg[n] = nc.dram_tensor(n, [T, D], F32, kind="ExternalOutput").ap()
        dbg["dbg_mt"] = nc.dram_tensor("dbg_mt", [128, 8 * T], F32, kind="ExternalOutput").ap()
        dbg["dbg_s"] = nc.dram_tensor("dbg_s", [128, 16], F32, kind="ExternalOutput").ap()
        dbg["dbg_h"] = nc.dram_tensor("dbg_h", [128, 512], F32, kind="ExternalOutput").ap()
        dbg["dbg_a"] = nc.dram_tensor("dbg_a", [128, 512], F32, kind="ExternalOutput").ap()
    xpark = nc.dram_tensor("xpark", [128, NT * D], F32)
    kv_in = [nc.dram_tensor("kv_in%d" % i, [128, 1024], F32) for i in range(4)]
    kv_all = [nc.dram_tensor("kv_all%d" % i, [4 * 128, 1024], F32) for i in range(4)]
    st_in = nc.dram_tensor("st_in", [128, SW], F32)
    st_all = nc.dram_tensor("st_all", [4 * 128, SW], F32)

    with contextlib.ExitStack() as st:
        s = Sched(nc, st)
        AW = 52000
        arena = st.enter_context(nc.sbuf_tensor("arena", [128, AW], F32))
        PS = [st.enter_context(nc.psum_tensor("ps%d" % i, [128, 512], F32)) for i in range(7)]
        PSR = [s.res("ps%d" % i) for i in range(7)]
        PTB = st.enter_context(nc.psum_tensor("psT", [128, 1024], BF16))
        PTBR = s.res("psT")

        class Alloc:
            def __init__(self):
                self.p = 0

            def f32(self, n, parts=128):
                o = self.p
                self.p += n
                assert self.p <= AW, ("arena overflow", self.p)
                return arena[0:parts, o:o + n]

            def bf(self, n, parts=128):
                w = (n + 1) // 2
                o = self.p
                self.p += w
                assert self.p <= AW, ("arena overflow", self.p)
                return arena[0:parts, o:o + w].bitcast(BF16)

            def bf2(self, n, parts=128):
                w = (n + 1) // 2
                o = self.p
                self.p += w
                assert self.p <= AW, ("arena overflow", self.p)
                return arena[0:parts, o:o + w].bitcast(BF16), arena[0:parts, o:o + w]

        A = Alloc()

        def mm(out, lhsT, rhs, start, stop, rd, wr, inc=True):
            s.op("pe", lambda e: e.matmul(out, lhsT=lhsT, rhs=rhs, start=start, stop=stop), rd, wr, inc=inc)

        def tr(out, in_, rd, wr, inc=True):
            s.op("pe", lambda e: e.transpose(out, in_, identb[0:in_.shape[0], 0:in_.shape[0]]), rd + [Rcp], wr, inc=inc)

        def act(out, in_, func, rd, wr, scale=1.0, bias=0.0, accum=None):
            if accum is None:
                s.op("act", lambda e: e.activation(out=out, in_=in_, func=func, bias=bias, scale=scale), rd, wr)
            else:
                s.op("act", lambda e: e.activation(out=out, in_=in_, func=func, bias=bias, scale=scale, accum_out=accum), rd, wr)

        def tt(eng, out, in0, in1, op, rd, wr):
            s.op(eng, lambda e: e.tensor_tensor(out=out, in0=in0, in1=in1, op=op), rd, wr)

        def ts(eng, out, in0, s1, op0, rd, wr, s2=None, op1=None):
            if op1 is None:
                s.op(eng, lambda e: e.tensor_scalar(out=out, in0=in0, scalar1=s1, scalar2=None, op0=op0), rd, wr)
            else:
                s.op(eng, lambda e: e.tensor_scalar(out=out, in0=in0, scalar1=s1, scalar2=s2, op0=op0, op1=op1), rd, wr)

        def stt(eng, out, in0, scalar, in1, op0, op1, rd, wr):
            s.op(eng, lambda e: e.scalar_tensor_tensor(out=out, in0=in0, scalar=scalar, in1=in1, op0=op0, op1=op1), rd, wr)

        def cp(eng, out, in_, rd, wr):
            if eng == "act":
                s.op("act", lambda e: e.copy(out=out, in_=in_), rd, wr)
            else:
                s.op(eng, lambda e: e.tensor_copy(out=out, in_=in_), rd, wr)

        def recip(out, in_, rd, wr):
            s.op("dve", lambda e: e.reciprocal(out=out, in_=in_), rd, wr)

        def rstd_from_ss(buf, n, R):
            ts("dve", buf, buf, 1.0 / n, ALU.mult, [R], [R], s2=EPS, op1=ALU.add)
            s.op("act", lambda e: e.sqrt(out=buf, in_=buf), [R], [R])
            recip(buf, buf, [R], [R])

        X = A.f32(NT * D).rearrange("p (t d) -> p t d", d=D)
        XR = [s.res("x%d" % t) for t in range(NT)]
        gcols = A.f32(40)
        lbl = A.f32(16)
        chain = A.f32(40)
        rotm = A.f32(128)
        ones32 = A.f32(128)
        identb = A.bf(128)
        onesb = A.bf(128)
        maskb = A.bf(128, parts=64)
        small = A.f32(64)
        Rconst = s.res("const")
        Rsmall = s.res("small")
        s.dma("sp", gcols, gcols_d, writes=[Rconst])
        s.dma("sp", lbl, lbl_d, writes=[Rconst])
        s.dma("sp", chain, chain_d, writes=[Rconst])
        s.dma("sp", rotm, rot_d, writes=[Rconst])
        Rcp = s.res("constp")
        s.dma("pool", identb, ident_d, writes=[Rcp])
        s.dma("pool", maskb, mask_d, writes=[Rcp])
        s.op("dve", lambda e: e.memset(ones32, 1.0), [], [Rconst])
        s.op("dve", lambda e: e.memset(onesb, 1.0), [], [Rconst])
        xv = x_d.rearrange("(t p) d -> p t d", p=128)
        for t4 in range(4):
            Rl = s.res("xl%d" % t4)
            s.dma("sp", X[:, 4 * t4:4 * t4 + 4, :], xv[:, 4 * t4:4 * t4 + 4, :], writes=[Rl] + XR[4 * t4:4 * t4 + 4])
        pbase = A.p

        GC_F1, GC_MIX, GC_F2 = 0, 8, 16
        GC_AO, GC_Q, GC_K, GC_RO = 24, 28, 29, 30

        def norm_T(hT, RhT, gc0):
            p0 = A.p
            junk = A.bf(D)
            Rj = s.res("junk")
            xn = [A.bf(D), A.bf(D)]
            Rxn = [s.res("xn0"), s.res("xn1")]
            ss = small[:, 0:16]
            s.op("dve", lambda e: e.memset(ss, 0.0), [], [Rsmall])
            for t in range(NT):
                act(junk, X[:, t, :], AF.Square, [XR[t]], [Rj, Rsmall], accum=ss[:, t:t + 1])
            rstd_from_ss(ss, D, Rsmall)
            for t in range(NT):
                b = t % 2
                s.op("act", lambda e, b=b, t=t: e.mul(out=xn[b], in_=X[:, t, :], mul=ss[:, t:t + 1]), [XR[t], Rsmall], [Rxn[b]])
                for c in range(8):
                    tr(PTB[:, c * 128:(c + 1) * 128], xn[b][:, c * 128:(c + 1) * 128], [Rxn[b]], [PTBR], inc=(c == 7))
                g3 = gcols[:, gc0:gc0 + 8].unsqueeze(2).to_broadcast([128, 8, 128])
                tt("dve", hT[:, :, t * 128:(t + 1) * 128], PTB[:, :].rearrange("p (c k) -> p c k", k=128), g3,
                   ALU.mult, [PTBR, Rconst], [RhT[t // 4]])

        def ffn(w_in, w_out, gc0, tagn):
            p0 = A.p
            hT = A.bf(8 * T).rearrange("p (c t) -> p c t", t=T)
            RhT = [s.res("hT%d" % i) for i in range(4)]
            norm_T(hT, RhT, gc0)
            if debug and tagn == 1:
                dh = A.f32(512)
                Rdh = s.res("dh")
                cp("dve", dh, hT[:, 0, 0:512], [RhT[0]], [Rdh])
                s.dma("sp", dbg["dbg_h"], dh, reads=[Rdh], writes=[s.res("dbgh")])
                s.dma("sp", dbg["dbg_s"], small[:, 0:16], reads=[Rsmall], writes=[s.res("dbgs")])
            win = [A.bf(8 * 512).rearrange("p (k n) -> p k n", n=512) for _ in range(2)]
            Rwin = [s.res("win0"), s.res("win1")]
            wout = [A.bf(2 * D).rearrange("p (c n) -> p c n", n=D) for _ in range(2)]
            Rwout = [s.res("wout0"), s.res("wout1")]
            actT = [A.bf(2 * T).rearrange("p (c t) -> p c t", t=T) for _ in range(2)]
            RactT = [[s.res("actT%d_%d" % (b, q)) for q in range(NQ)] for b in range(2)]
            sg = [A.f32(512), A.f32(512)]
            Rsg = [s.res("sg0"), s.res("sg1")]
            w_in_v = w_in.rearrange("(k p) n -> p k n", p=128)
            w_out_v = w_out.rearrange("(c p) n -> p c n", p=128)
            it = 0
            for g in range(NG):
                b = g % 2
                s.dma("pool", win[b][:, :, 0:256], w_in_v[:, :, g * 256:(g + 1) * 256], writes=[Rwin[b]])
                s.dma("pool", win[b][:, :, 256:512], w_in_v[:, :, DFF + g * 256:DFF + (g + 1) * 256], writes=[Rwin[b]])
                s.dma("pool", wout[b][:, :, :], w_out_v[:, 2 * g:2 * g + 2, :], writes=[Rwout[b]])
                for q in range(NQ):
                    for c in range(2):
                        pg, pu = it % 2, 2 + it % 2
                        for k in range(8):
                            mm(PS[pg][:, :], win[b][:, k, c * 128:(c + 1) * 128], hT[:, k, q * 512:(q + 1) * 512],
                               k == 0, k == 7, [Rwin[b], RhT[q]], [PSR[pg]], inc=(k == 7))
                        for k in range(8):
                            mm(PS[pu][:, :], win[b][:, k, 256 + c * 128:256 + (c + 1) * 128], hT[:, k, q * 512:(q + 1) * 512],
                               k == 0, k == 7, [Rwin[b], RhT[q]], [PSR[pu]], inc=(k == 7))
                        sb = it % 2
                        act(sg[sb], PS[pg][:, :], AF.Silu, [PSR[pg]], [Rsg[sb]])
                        tt("dve", actT[b][:, c, q * 512:(q + 1) * 512], sg[sb], PS[pu][:, :], ALU.mult,
                           [Rsg[sb], PSR[pu]], [RactT[b][q]])
                        it += 1
                        if debug and tagn == 1 and g == 0 and q == 0 and c == 0:
                            da = A.f32(512)
                            Rda = s.res("da")
                            cp("dve", da, actT[b][:, 0, 0:512], [RactT[b][0]], [Rda])
                            s.dma("sp", dbg["dbg_a"], da, reads=[Rda], writes=[s.res("dbga")])
                for t in range(NT):
                    for h in range(2):
                        pd = 4 + (2 * t + h) % 2
                        for c in range(2):
                            mm(PS[pd][:, :], actT[b][:, c, t * 128:(t + 1) * 128], wout[b][:, c, h * 512:(h + 1) * 512],
                               c == 0, c == 1, [RactT[b][t // 4], Rwout[b]], [PSR[pd]], inc=(c == 1))
                        stt("dve", X[:, t, h * 512:(h + 1) * 512], PS[pd][:, :], 0.5, X[:, t, h * 512:(h + 1) * 512],
                            ALU.mult, ALU.add, [PSR[pd], XR[t]], [XR[t]])
            s.barrier()
            A.p = p0

        def dump_x(name):
            if debug:
                Rd = s.res(name)
                s.dma("sp", dbg[name].rearrange("(t p) d -> p t d", p=128), X[:, :, :], reads=XR, writes=[Rd])
                s.barrier()

        ffn(w_in1, w_out1, GC_F1, 1)
        dump_x("dbg_x1")

        if stop_after != "ffn1":
            mixer(nc, s, A, locals())

        if stop_after is None:
            ffn(w_in2, w_out2, GC_F2, 2)
        p0 = A.p
        fg = A.f32(D)
        Rfg = s.res("fg")
        s.dma("sp", fg, fgain_d.partition_broadcast(128)[:, 0, :], writes=[Rfg])
        junk = A.bf(D)
        Rj = s.res("junkf")
        ss = small[:, 0:16]
        s.op("dve", lambda e: e.memset(ss, 0.0), [], [Rsmall])
        for t in range(NT):
            act(junk, X[:, t, :], AF.Square, [XR[t]], [Rj, Rsmall], accum=ss[:, t:t + 1])
        rstd_from_ss(ss, D, Rsmall)
        ov = out_d.rearrange("(t p) d -> p t d", p=128)
        Rout = s.res("out")
        for t in range(NT):
            stt("dve", X[:, t, :], X[:, t, :], ss[:, t:t + 1], fg, ALU.mult, ALU.mult, [XR[t], Rsmall, Rfg], [XR[t]])
            if t % 4 == 3:
                s.dma("sp", ov[:, t - 3:t + 1, :], X[:, t - 3:t + 1, :], reads=XR[t - 3:t + 1], writes=[Rout])
        s.barrier()
        s.emit()
    return nc


def mixer(nc, s, A, L):
    X, XR, PS, PSR, PTB, PTBR = L["X"], L["XR"], L["PS"], L["PSR"], L["PTB"], L["PTBR"]
    gcols, lbl, chain, rotm, ones32, identb, onesb, maskb, small = (L[k] for k in (
        "gcols", "lbl", "chain", "rotm", "ones32", "identb", "onesb", "maskb", "small"))
    Rconst, Rsmall = L["Rconst"], L["Rsmall"]
    mm, tr, act, tt, ts, stt, cp, recip, rstd_from_ss = (L[k] for k in (
        "mm", "tr", "act", "tt", "ts", "stt", "cp", "recip", "rstd_from_ss"))
    norm_T = L["norm_T"]
    debug, dbg = L["debug"], L["dbg"]
    w_inm, w_outm, cos_d, sin_d, qkrow_d = L["w_inm"], L["w_outm"], L["cos_d"], L["sin_d"], L["qkrow_d"]
    xpark, kv_in, kv_all, st_in, st_all = L["xpark"], L["kv_in"], L["kv_all"], L["st_in"], L["st_all"]
    GC_MIX, GC_AO, GC_Q, GC_K, GC_RO = L["GC_MIX"], L["GC_AO"], L["GC_Q"], L["GC_K"], L["GC_RO"]
    res = s.res
    p_mix = A.p
    hT = A.bf(8 * T).rearrange("p (c t) -> p c t", t=T)
    RhT = [res("mhT%d" % i) for i in range(4)]
    p_after_hT = A.p
    Rpark = res("xpark")
    xpv = xpark.ap().rearrange("p (t d) -> p t d", d=D)
    s.dma("sp", xpv, X[:, :, :], reads=XR, writes=[Rpark])
    norm_T(hT, RhT, GC_MIX)
    s.barrier()
    A.p = p_after_hT
    save_p = A.p
    A.p = 0
    QT = A.bf(4 * T).rearrange("p (h t) -> p h t", t=T)
    RQT = [res("QT%d" % h) for h in range(4)]
    MT = A.bf(8 * T).rearrange("p (c t) -> p c t", t=T)
    RMT = [res("MT%d" % i) for i in range(8)]
    STL = A.f32(SW)
    RSTL = res("STL")
    negB = A.f32(1)
    lbc = A.f32(16)
    Rlb = res("lb")
    smask = A.bf(T)
    Rsm = res("smask")
    assert A.p <= NT * D
    A.p = save_p
    p_small = A.p

    s.op("dve", lambda e: e.memset(smask, 1.0), [], [Rsm])
    s.op("dve", lambda e: e.memset(smask.rearrange("p (j c) -> p j c", c=C)[:, :, 0:1], 0.0), [], [Rsm])
    for d_ in range(2):
        tt("dve", lbc[:, d_ * 4:d_ * 4 + 4], lbl[:, d_ * 8:d_ * 8 + 4], lbl[:, d_ * 8 + 4:d_ * 8 + 8], ALU.subtract, [Rconst], [Rlb])
    act(lbc[:, 8:16], lbc[:, 0:8], AF.Sigmoid, [Rlb], [Rlb], scale=-1.0)

    qkrow = A.f32(256, parts=1)
    Rqk = res("qkrow")
    s.dma("sp", qkrow, qkrow_d, writes=[Rqk])
    mx = A.f32(4, parts=1)
    s.op("dve", lambda e: e.reduce_max(out=mx[:, 0:1], in_=qkrow[:, 0:128], axis=AX.X, apply_absolute_value=True), [Rqk], [Rqk])
    s.op("dve", lambda e: e.reduce_max(out=mx[:, 1:2], in_=qkrow[:, 128:256], axis=AX.X, apply_absolute_value=True), [Rqk], [Rqk])
    stt("dve", mx[:, 2:3], mx[:, 0:1], -float(np.sqrt(128.0)), mx[:, 1:2], ALU.mult, ALU.mult, [Rqk], [Rqk])
    mm(PS[6][:, 0:1], ones32[0:1, :], mx[:, 2:3], True, True, [Rqk, Rconst], [PSR[6]])
    cp("dve", negB, PS[6][:, 0:1], [PSR[6]], [Rlb])
    p_small = A.p

    w_v = w_inm.rearrange("(k p) n -> p k n", p=128)

    wq = [A.bf(8 * 256).rearrange("p (k n) -> p k n", n=256) for _ in range(2)]
    Rwq = [res("wq0"), res("wq1")]
    cosT = A.f32(T)
    sinT = A.f32(T)
    Rrope = res("rope")
    s.dma("sp", cosT, cos_d, writes=[Rrope])
    s.dma("sp", sinT, sin_d, writes=[Rrope])
    qfb = [A.f32(T), A.f32(T)]
    sqb = [A.f32(T), A.f32(T)]
    t1b = [A.f32(T), A.f32(T)]
    Rqfb = [[res("qf%d_%d" % (b, q)) for q in range(NQ)] for b in range(2)]
    Rsqb = [[res("sq%d_%d" % (b, q)) for q in range(NQ)] for b in range(2)]
    Rt1b = [[res("t1%d_%d" % (b, q)) for q in range(NQ)] for b in range(2)]
    KTo_b, KTo_w = A.bf2(2 * T)
    KTo = KTo_b.rearrange("p (h t) -> p h t", t=T)
    RKTo = res("KTo")
    Vo_b, Vo_w = A.bf2(NT * 256)
    Vo = Vo_b.rearrange("p (t n) -> p t n", n=256)
    RVo = res("Vo")
    QS = [slice(q * 512, (q + 1) * 512) for q in range(NQ)]
    pi = 0
    for pair in range(3):
        b = pair % 2
        s.dma("pool", wq[b][:, :, :], w_v[:, :, pair * 256:(pair + 1) * 256], writes=[Rwq[b]])
        for c in range(2):
            ch = pair * 2 + c
            bs = ch % 2
            qf, sq, t1 = qfb[bs], sqb[bs], t1b[bs]
            Rqf, Rsq, Rt1 = Rqfb[bs], Rsqb[bs], Rt1b[bs]
            gcol = gcols[:, GC_Q:GC_Q + 1] if ch < 4 else gcols[:, GC_K:GC_K + 1]
            for q, sl in enumerate(QS):
                pp = pi % 2
                pi += 1
                for k in range(8):
                    mm(PS[pp][:, :], wq[b][:, k, c * 128:(c + 1) * 128], hT[:, k, sl],
                       k == 0, k == 7, [Rwq[b], RhT[q]], [PSR[pp]], inc=(k == 7))
                cp("act", qf[:, sl], PS[pp][:, :], [PSR[pp]], [Rqf[q]])
                act(t1[:, sl], PS[pp][:, :], AF.Square, [PSR[pp]], [Rt1[q]])
                po1 = 2 + 2 * (q % 2)
                mm(PS[po1][:, :], ones32, t1[:, sl], True, True, [Rt1[q], Rconst], [PSR[po1]])
                cp("dve", sq[:, sl], PS[po1][:, :], [PSR[po1]], [Rsq[q]])
            for q, sl in enumerate(QS):
                rstd_from_ss(sq[:, sl], 128, Rsq[q])
            for q, sl in enumerate(QS):
                stt("dve", qf[:, sl], qf[:, sl], gcol, sq[:, sl], ALU.mult, ALU.mult, [Rqf[q], Rsq[q], Rconst], [Rqf[q]])
            for q, sl in enumerate(QS):
                po2 = 3 + 2 * (q % 2)
                mm(PS[po2][:, :], rotm, qf[:, sl], True, True, [Rqf[q], Rconst], [PSR[po2]])
                tt("dve", t1[:, sl], PS[po2][:, :], sinT[:, sl], ALU.mult, [PSR[po2], Rrope], [Rt1[q]])
            for q, sl in enumerate(QS):
                tt("dve", sq[:, sl], qf[:, sl], cosT[:, sl], ALU.mult, [Rqf[q], Rrope], [Rsq[q]])
            for q, sl in enumerate(QS):
                if ch < 4:
                    tt("dve", QT[:, ch, sl], sq[:, sl], t1[:, sl], ALU.add, [Rsq[q], Rt1[q]], [RQT[ch]])
                else:
                    tt("dve", KTo[:, ch - 4, sl], sq[:, sl], t1[:, sl], ALU.add, [Rsq[q], Rt1[q]], [RKTo])
    s.dma("pool", wq[1][:, :, :], w_v[:, :, 768:1024], writes=[Rwq[1]])
    for t in range(NT):
        pp = t % 2
        for k in range(8):
            mm(PS[pp][:, 0:256], hT[:, k, t * 128:(t + 1) * 128], wq[1][:, k, :], k == 0, k == 7,
               [Rwq[1], RhT[t // 4]], [PSR[pp]], inc=(k == 7))
        cp("act", Vo[:, t, :], PS[pp][:, 0:256], [PSR[pp]], [RVo])
    Rkvin = [res("kvin%d" % i) for i in range(4)]
    Rkvall = [res("kvall%d" % i) for i in range(4)]
    for i in range(4):
        srcw = (KTo_w if i < 2 else Vo_w)[:, (i % 2) * 1024:(i % 2 + 1) * 1024]
        s.dma("sp", kv_in[i].ap(), srcw, reads=[RKTo if i < 2 else RVo], writes=[Rkvin[i]])
        s.custom("pool", lambda e, i=i: e.collective_compute("AllGather", ALU.bypass, replica_groups=[[0, 1, 2, 3], [4, 5, 6, 7]],
                                                             ins=[kv_in[i].ap().opt()], outs=[kv_all[i].ap().opt()]),
                 Rkvall[i], 1, reads=[Rkvin[i]], writes=[Rkvall[i]])
    s.barrier()
    A.p = p_small

    _hres = {}

    def hgrn_head(h, full, SA=None, coef=None, RSA=None, Rcoef=None):
        p0 = A.p

        def res(name):
            k = (full, name)
            if k not in _hres:
                _hres[k] = s.res(name)
            return _hres[k]
        wr = A.bf(8 * 640).rearrange("p (k n) -> p k n", n=640)
        Rwr = res("wr")
        kf, gf, bb = A.f32(T), A.f32(T), A.f32(T)
        Rkf = [res("kf%d" % q) for q in range(NQ)]
        Rgf = [res("gf%d" % q) for q in range(NQ)]
        Rbb = [res("bb%d" % q) for q in range(NQ)]
        Lq = A.f32(4)
        RLq = res("Lq")
        totc = A.f32(NCH)
        Rtot = res("totc")
        EL = A.f32(2 * NCH).rearrange("p (a j) -> p a j", j=NCH)
        REL = res("EL")
        KTl = [A.bf(T) for _ in range(2)]
        RKTl = [res("KTl0"), res("KTl1")]
        KHb = A.bf(T)
        RKHb = [res("KHb%d" % q) for q in range(NQ)]
        KH = [A.bf(NCH * 128, parts=64).rearrange("p (j d) -> p j d", d=128) for _ in range(2)]
        RKH = [res("KH0"), res("KH1")]
        Vt = A.bf(NCH * 128, parts=64).rearrange("p (j d) -> p j d", d=128)
        RVt = res("Vt")
        Sst = [A.f32(128), A.f32(128)]
        Sbf = [A.bf(128), A.bf(128)]
        RS = [res("S0"), res("S1")]
        RSb = [res("Sb0"), res("Sb1")]
        if full:
            qh = A.f32(T)
            Rqh = res("qh")
            QF = [A.bf(T), A.bf(T)]
            RQF = [res("QF0"), res("QF1")]
            GTh = A.bf(T)
            RGT = res("GTh")
            oT = kf
            RoTq = Rkf
            Am = [[A.bf(64, parts=64), A.bf(64, parts=64)] for _ in range(2)]
            RAm = [[res("Am%d_%d" % (d_, i)) for i in range(2)] for d_ in range(2)]
            Sbf2 = [[A.bf(128), A.bf(128)] for _ in range(2)]
            RSb2 = [[res("Sb%d_%d" % (d_, i)) for i in range(2)] for d_ in range(2)]
            SAh = A.f32(4 * 256).rearrange("p (r w) -> p r w", w=256)
            RSAh = res("SAh")
            s.dma("sp", SAh, st_all.ap().rearrange("(r p) w -> p r w", p=128)[:, :, h * 256:(h + 1) * 256],
                  reads=[RSA], writes=[RSAh])
        for i in range(5):
            s.dma("pool", wr[:, :, i * 128:(i + 1) * 128], w_v[:, :, 1024 + 512 * i + h * 128:1024 + 512 * i + (h + 1) * 128],
                  writes=[Rwr])
        pi = 0
        if full:
            for q in range(NQ):
                sl = slice(q * 512, (q + 1) * 512)
                pp = pi % 2
                pi += 1
                for k in range(8):
                    mm(PS[pp][:, :], wr[:, k, 0:128], hT[:, k, sl], k == 0, k == 7, [Rwr, RhT[q]], [PSR[pp]], inc=(k == 7))
                cp("dve", qh[:, sl], PS[pp][:, :], [PSR[pp]], [Rqh])
                pp = pi % 2
                pi += 1
                for k in range(8):
                    mm(PS[pp][:, :], wr[:, k, 512:640], hT[:, k, sl], k == 0, k == 7, [Rwr, RhT[q]], [PSR[pp]], inc=(k == 7))
                act(GTh[:, sl], PS[pp][:, :], AF.Silu, [PSR[pp]], [RGT])
        QS = [slice(q * 512, (q + 1) * 512) for q in range(NQ)]
        for d_ in range(2):
            a = h * 2 + d_
            oml = lbc[:, 8 + d_ * 4 + h:8 + d_ * 4 + h + 1]
            lastc = (C - 1) if d_ == 0 else 0
            for q, sl in enumerate(QS):
                pp = pi % 2
                pi += 1
                for k in range(8):
                    mm(PS[pp][:, :], wr[:, k, 128 + d_ * 128:256 + d_ * 128], hT[:, k, sl], k == 0, k == 7,
                       [Rwr, RhT[q]], [PSR[pp]], inc=(k == 7))
                act(kf[:, sl], PS[pp][:, :], AF.Sigmoid, [PSR[pp]], [Rkf[q]], scale=-1.0)
            for q, sl in enumerate(QS):
                ts("dve", kf[:, sl], kf[:, sl], oml, ALU.mult, [Rkf[q], Rlb], [Rkf[q]])
            for q, sl in enumerate(QS):
                act(gf[:, sl], kf[:, sl], AF.Ln, [Rkf[q]], [Rgf[q]], scale=-1.0, bias=1.0)
            for q, sl in enumerate(QS):
                if not full:
                    s.op("dve", lambda e, q=q, sl=sl: e.tensor_reduce(out=Lq[:, q:q + 1], in_=gf[:, sl], axis=AX.X, op=ALU.add),
                         [Rgf[q]], [RLq])
                s.op("dve", lambda e, sl=sl: e.tensor_tensor_scan(out=bb[:, sl], data0=smask[:, sl], data1=gf[:, sl], initial=0.0,
                                                                  op0=ALU.mult, op1=ALU.add), [Rsm, Rgf[q]], [Rbb[q]])
                if d_ == 1:
                    bq3 = bb[:, sl].rearrange("p (j c) -> p j c", c=C)
                    tt("dve", gf[:, sl], bb[:, sl], gf[:, sl], ALU.subtract, [Rbb[q], Rgf[q]], [Rgf[q]])
                    cp("dve", totc[:, q * 8:(q + 1) * 8], bq3[:, :, C - 1], [Rbb[q]], [Rtot])
                    tt("dve", bq3, totc[:, q * 8:(q + 1) * 8].unsqueeze(2).to_broadcast([128, 8, C]),
                       gf[:, sl].rearrange("p (j c) -> p j c", c=C), ALU.subtract, [Rgf[q], Rtot], [Rbb[q]])
            if not full:
                s.op("dve", lambda e, a=a: e.tensor_reduce(out=STL[:, 1024 + a:1025 + a], in_=Lq, axis=AX.X, op=ALU.add),
                     [RLq], [RSTL])
            for q, sl in enumerate(QS):
                act(gf[:, sl], bb[:, sl], AF.Exp, [Rbb[q]], [Rgf[q]])
            for q, sl in enumerate(QS):
                cp("dve", EL[:, d_, q * 8:(q + 1) * 8], gf[:, sl].rearrange("p (j c) -> p j c", c=C)[:, :, lastc], [Rgf[q]], [REL])
                if full:
                    tt("dve", QF[d_][:, sl], qh[:, sl], gf[:, sl], ALU.mult, [Rqh, Rgf[q]], [RQF[d_]])
            for q, sl in enumerate(QS):
                act(gf[:, sl], bb[:, sl], AF.Exp, [Rbb[q]], [Rgf[q]], scale=-1.0)
            for q, sl in enumerate(QS):
                tt("dve", KTl[d_][:, sl], kf[:, sl], gf[:, sl], ALU.mult, [Rkf[q], Rgf[q]], [RKTl[d_]])
                tt("dve", KHb[:, sl].rearrange("p (j c) -> p j c", c=C), KTl[d_][:, sl].rearrange("p (j c) -> p j c", c=C),
                   EL[:, d_, q * 8:(q + 1) * 8].unsqueeze(2).to_broadcast([128, 8, C]), ALU.mult, [RKTl[d_], REL], [RKHb[q]])
                for j in range(q * 8, q * 8 + 8):
                    tr(PTB[0:64, (j % 8) * 128:(j % 8 + 1) * 128], KHb[:, j * C:(j + 1) * C], [RKHb[q]], [PTBR], inc=(j % 8 == 7))
                cp("act", KH[d_][:, q * 8:q * 8 + 8, :], PTB[0:64, :].rearrange("p (j d) -> p j d", d=128), [PTBR], [RKH[d_]])
        for j in range(NCH):
            pp = j % 2
            for k in range(8):
                mm(PS[pp][0:64, 0:128], hT[:, k, j * C:(j + 1) * C], wr[:, k, 384:512], k == 0, k == 7,
                   [Rwr, RhT[j // 8]], [PSR[pp]], inc=(k == 7))
            cp("act", Vt[:, j, :], PS[pp][0:64, 0:128], [PSR[pp]], [RVt])
        if full:
            s.op("dve", lambda e: e.memset(oT, 0.0), [], RoTq)
        for d_ in range(2):
            a = h * 2 + d_
            if full:
                for p in range(4):
                    Sp = SAh[:, p, d_ * 128:(d_ + 1) * 128]
                    if p == 0:
                        ts("dve", Sst[d_], Sp, coef[:, p, a:a + 1], ALU.mult, [RSAh, Rcoef], [RS[d_]])
                    else:
                        stt("dve", Sst[d_], Sp, coef[:, p, a:a + 1], Sst[d_], ALU.mult, ALU.add, [RSAh, Rcoef, RS[d_]], [RS[d_]])
                cp("act", Sbf2[d_][0], Sst[d_], [RS[d_]], [RSb2[d_][0]])
        def jof(step, d_):
            return step if d_ == 0 else NCH - 1 - step

        def stageA(step, d_):
            j = jof(step, d_)
            cs = slice(j * C, (j + 1) * C)
            pa = 2 + d_
            ca = (step % 2) * 64
            mm(PS[pa][0:64, ca:ca + 64], KTl[d_][:, cs], QF[d_][:, cs], True, True, [RKTl[d_], RQF[d_]], [PSR[pa]])
            tt("dve", Am[d_][step % 2], PS[pa][0:64, ca:ca + 64], maskb[:, d_ * 64:(d_ + 1) * 64], ALU.mult,
               [PSR[pa], L["Rcp"]], [RAm[d_][step % 2]])

        def stageKV(step, d_):
            a = h * 2 + d_
            j = jof(step, d_)
            pk = 6
            mm(PS[pk][:, (d_ * 128):(d_ * 128 + 128)], KH[d_][:, j, :], Vt[:, j, :], True, True, [RKH[d_], RVt], [PSR[pk]])
            if step == 0 and not full:
                cp("dve", Sst[d_], PS[pk][:, d_ * 128:d_ * 128 + 128], [PSR[pk]], [RS[d_]])
            else:
                stt("dve", Sst[d_], Sst[d_], EL[:, d_, j:j + 1], PS[pk][:, d_ * 128:d_ * 128 + 128], ALU.mult, ALU.add,
                    [RS[d_], REL, PSR[pk]], [RS[d_]])
            if step < NCH - 1:
                if full:
                    cp("act", Sbf2[d_][(step + 1) % 2], Sst[d_], [RS[d_]], [RSb2[d_][(step + 1) % 2]])
            elif not full:
                cp("act", STL[:, a * 128:(a + 1) * 128], Sst[d_], [RS[d_]], [RSTL])

        def stageO(step, d_):
            j = jof(step, d_)
            cs = slice(j * C, (j + 1) * C)
            po = 4 + d_
            oc = (step % 8) * C
            mm(PS[po][:, oc:oc + C], Vt[:, j, :], Am[d_][step % 2], True, False, [RVt, RAm[d_][step % 2]], [PSR[po]], inc=False)
            mm(PS[po][:, oc:oc + C], Sbf2[d_][step % 2], QF[d_][:, cs], False, True, [RSb2[d_][step % 2], RQF[d_]], [PSR[po]])
            if step % 8 == 7:
                if d_ == 0:
                    j0 = step - 7
                    tt("dve", oT[:, j0 * C:(j0 + 8) * C], oT[:, j0 * C:(j0 + 8) * C], PS[po][:, :], ALU.add,
                       [PSR[po], RoTq[j0 // 8]], [RoTq[j0 // 8]])
                else:
                    src = PS[po][:, :].rearrange("p (s c) -> p s c", c=C)
                    for s8 in range(8):
                        jj = j + 7 - s8
                        tt("dve", oT[:, jj * C:(jj + 1) * C], oT[:, jj * C:(jj + 1) * C], src[:, s8, :], ALU.add,
                           [PSR[po], RoTq[jj // 8]], [RoTq[jj // 8]])

        if full:
            for d_ in range(2):
                stageA(0, d_)
        for step in range(NCH):
            for d_ in range(2):
                stageKV(step, d_)
                if full:
                    if step + 1 < NCH:
                        stageA(step + 1, d_)
                    stageO(step, d_)
        if full:
            sqo = gf
            rso = bb
            for q in range(NQ):
                sl = slice(q * 512, (q + 1) * 512)
                act(sqo[:, sl], oT[:, sl], AF.Square, [RoTq[q]], [Rgf[q]])
                mm(PS[0][:, :], ones32, sqo[:, sl], True, True, [Rgf[q], Rconst], [PSR[0]])
                cp("dve", rso[:, sl], PS[0][:, :], [PSR[0]], [Rbb[q]])
                rstd_from_ss(rso[:, sl], 128, Rbb[q])
                stt("dve", oT[:, sl], oT[:, sl], gcols[:, GC_RO:GC_RO + 1], rso[:, sl], ALU.mult, ALU.mult,
                    [RoTq[q], Rbb[q], Rconst], [RoTq[q]])
                tt("dve", MT[:, 4 + h, sl], oT[:, sl], GTh[:, sl], ALU.mult, [RoTq[q], RGT], [RMT[4 + h]])
        if h == 3:
            s.barrier()
        A.p = p0

    s.op("dve", lambda e: e.memset(STL, 0.0), [], [RSTL])
    for h in range(4):
        hgrn_head(h, False)
    Rstin = res("stin")
    Rstall = res("stall")
    s.dma("sp", st_in.ap(), STL, reads=[RSTL], writes=[Rstin])
    s.custom("pool", lambda e: e.collective_compute("AllGather", ALU.bypass, replica_groups=[[0, 1, 2, 3], [4, 5, 6, 7]],
                                                    ins=[st_in.ap().opt()], outs=[st_all.ap().opt()]),
             Rstall, 1, reads=[Rstin], writes=[Rstall])
    s.barrier()
    A.p = p_small

    KT_b, KT_w = A.bf2(2 * 8192)
    KT = KT_b.rearrange("p (h t) -> p h t", t=8192)
    VA_b, VA_w = A.bf2(64 * 256)
    VA = VA_b.rearrange("p (c n) -> p c n", n=256)
    RKT, RVA = res("KT"), res("VA")
    for p in range(4):
        for g in range(2):
            s.dma("sp", KT_w[:, g * 4096 + p * 1024:g * 4096 + (p + 1) * 1024], kv_all[g].ap()[p * 128:(p + 1) * 128, :],
                  reads=[Rkvall[g]], writes=[RKT])
        for i in (2, 3):
            c0 = (p * 16 + (i - 2) * 8) * 128
            s.dma("sp", VA_w[:, c0:c0 + 1024], kv_all[i].ap()[p * 128:(p + 1) * 128, :], reads=[Rkvall[i]], writes=[RVA])
    PTs = [A.bf(512) for _ in range(3)]
    RPT = [res("PT%d" % i) for i in range(3)]
    AT = A.f32(4 * 512).rearrange("p (h t) -> p h t", t=512)
    RAT = res("AT")
    rl = A.f32(512)
    Rrl = res("rl")
    sqa = A.f32(512)
    Rsqa = res("sqa")
    rsa = A.f32(512)
    Rrsa = res("rsa")
    scale = float(128.0 ** -0.5)
    iters = [(q, h, kc) for q in range(NQ) for h in range(4) for kc in range(64)]
    SB = [0, 1, 6]
    LA = 2

    def issue_S(i):
        q, h, kc = iters[i]
        g = h // 2
        pb = SB[i % 3]
        pt = i % 3
        mm(PS[pb][:, :], KT[:, g, kc * 128:(kc + 1) * 128], QT[:, h, q * 512:(q + 1) * 512], True, True,
           [RKT, RQT[h]], [PSR[pb]])
        act(PTs[pt], PS[pb][:, :], AF.Exp, [PSR[pb], Rlb], [RPT[pt]], scale=scale, bias=negB)

    def issue_PV(i):
        q, h, kc = iters[i]
        g = h // 2
        sl = slice(q * 512, (q + 1) * 512)
        pt = i % 3
        po, pl = 2 + (q * 4 + h) % 2, 4 + (q * 4 + h) % 2
        mm(PS[po][:, :], VA[:, kc, g * 128:(g + 1) * 128], PTs[pt], kc == 0, kc == 63, [RVA, RPT[pt]], [PSR[po]], inc=False)
        mm(PS[pl][:, :], onesb, PTs[pt], kc == 0, kc == 63, [Rconst, RPT[pt]], [PSR[pl]])
        if kc == 63:
            recip(rl, PS[pl][:, :], [PSR[pl]], [Rrl])
            tt("dve", AT[:, h, :], PS[po][:, :], rl, ALU.mult, [PSR[po], Rrl], [RAT])
            if h == 3:
                for hh in range(4):
                    act(sqa, AT[:, hh, :], AF.Square, [RAT], [Rsqa])
                    mm(PS[3][:, :], ones32, sqa, hh == 0, hh == 3, [Rsqa, Rconst], [PSR[3]], inc=True)
                cp("dve", rsa, PS[3][:, :], [PSR[3]], [Rrsa])
                rstd_from_ss(rsa, 512, Rrsa)
                for hh in range(4):
                    stt("dve", MT[:, hh, sl], AT[:, hh, :], gcols[:, GC_AO + hh:GC_AO + hh + 1], rsa, ALU.mult, ALU.mult,
                        [RAT, Rrsa, Rconst], [RMT[hh]])

    for i in range(min(LA, len(iters))):
        issue_S(i)
    for i in range(len(iters)):
        if i + LA < len(iters):
            issue_S(i + LA)
        issue_PV(i)
    s.barrier()
    A.p = p_small

    SA = A.f32(4 * 8).rearrange("p (r w) -> p r w", w=8)
    RSA = Rstall
    RSAL = res("SAL")
    s.dma("sp", SA, st_all.ap().rearrange("(r p) w -> p r w", p=128)[:, :, 1024:1032], reads=[Rstall], writes=[RSAL])
    cl = A.f32(8)
    Rcl = res("cl")
    coef = A.f32(4 * 8).rearrange("p (r a) -> p r a", a=8)
    Rcoef = res("coef")
    for d_ in range(2):
        for p in range(4):
            base = d_ * 20 + p * 5
            for j in range(4):
                Lj = SA[:, j, d_:8:2]
                wcol = chain[:, base + j:base + j + 1]
                if j == 0:
                    ts("dve", cl[:, 0:4], Lj, wcol, ALU.mult, [RSAL, Rconst], [Rcl])
                else:
                    stt("dve", cl[:, 0:4], Lj, wcol, cl[:, 0:4], ALU.mult, ALU.add, [RSAL, Rconst, Rcl], [Rcl])
            act(cl[:, 4:8], cl[:, 0:4], AF.Exp, [Rcl], [Rcl])
            ts("dve", coef[:, p, d_:8:2], cl[:, 4:8], chain[:, base + 4:base + 5], ALU.mult, [Rcl, Rconst], [Rcoef])
    for h in range(4):
        hgrn_head(h, True, SA=SA, coef=coef, RSA=RSA, Rcoef=Rcoef)
    A.p = p_small
    if debug:
        dtmp = A.f32(T)
        Rdt = res("dtmp")
        Rdm = res("dbgmt")
        for c8 in range(8):
            cp("dve", dtmp, MT[:, c8, :], [RMT[c8]], [Rdt])
            s.dma("sp", dbg["dbg_mt"][:, c8 * T:(c8 + 1) * T], dtmp, reads=[Rdt], writes=[Rdm])
        s.barrier()
        A.p = p_small
    A.p = p_mix
    X2 = A.f32(NT * D).rearrange("p (t d) -> p t d", d=D)
    RX2 = [res("x2_%d" % t) for t in range(NT)]
    for t4 in range(4):
        s.dma("sp", X2[:, 4 * t4:4 * t4 + 4, :], xpv[:, 4 * t4:4 * t4 + 4, :], reads=[Rpark], writes=RX2[4 * t4:4 * t4 + 4])
    wo = A.bf(8 * D).rearrange("p (c n) -> p c n", n=D)
    Rwo = res("wo")
    s.dma("pool", wo, w_outm.rearrange("(c p) n -> p c n", p=128), writes=[Rwo])
    for t in range(NT):
        for hh in range(2):
            pd = (2 * t + hh) % 2
            for c8 in range(8):
                mm(PS[pd][:, :], MT[:, c8, t * 128:(t + 1) * 128], wo[:, c8, hh * 512:(hh + 1) * 512], c8 == 0, c8 == 7,
                   [RMT[c8], Rwo], [PSR[pd]], inc=(c8 == 7))
            tt("dve", X2[:, t, hh * 512:(hh + 1) * 512], X2[:, t, hh * 512:(hh + 1) * 512], PS[pd][:, :], ALU.add,
               [RX2[t], PSR[pd]], [RX2[t]])
    s.barrier()
    for t in range(NT):
        cp(("act", "dve")[t % 2], X[:, t, :], X2[:, t, :], [RX2[t]], [XR[t]])
    s.barrier()
    A.p = p_mix
    L["dump_x"]("dbg_x2")


_NC_CACHE = {}


def _consts(core):
    r = core % 4
    tok0 = r * T
    pos = np.arange(tok0, tok0 + T)
    rows = (pos // 64).astype(np.float32)
    cols = (pos % 64).astype(np.float32)
    inv = (np.float32(10000.0) ** (-np.arange(0, 64, 2, dtype=np.float32) / np.float32(64))).astype(np.float32)
    ang_r = (rows[:, None] * inv[None, :]).astype(np.float32)
    ang_c = (cols[:, None] * inv[None, :]).astype(np.float32)
    cr, sr, cc, sc = np.cos(ang_r), np.sin(ang_r), np.cos(ang_c), np.sin(ang_c)
    cosT = np.concatenate([cr, cr, cc, cc], axis=1).T.astype(np.float32)
    sinT = np.concatenate([sr, sr, sc, sc], axis=1).T.astype(np.float32)
    rot = np.zeros((128, 128), np.float32)
    for i in range(32):
        rot[32 + i, i] = -1.0
        rot[i, 32 + i] = 1.0
        rot[96 + i, 64 + i] = -1.0
        rot[64 + i, 96 + i] = 1.0
    mask = np.zeros((64, 128), np.float32)
    sI, tI = np.meshgrid(np.arange(64), np.arange(64), indexing="ij")
    mask[:, 0:64] = (sI <= tI)
    mask[:, 64:128] = (sI >= tI)
    ch = np.zeros((2, 4, 5), np.float32)
    for p in range(4):
        if p < r:
            ch[0, p, 4] = 1.0
            for j in range(4):
                ch[0, p, j] = 1.0 if (p < j < r) else 0.0
        if p > r:
            ch[1, p, 4] = 1.0
            for j in range(4):
                ch[1, p, j] = 1.0 if (r < j < p) else 0.0
    chain = np.broadcast_to(ch.reshape(1, 40), (128, 40)).astype(np.float32)
    return cosT, sinT, rot, mask, np.ascontiguousarray(chain)


def make_in_maps(inputs):
    f = lambda k: np.ascontiguousarray(np.asarray(inputs[k], dtype=np.float32))
    x = f("x")
    gc = np.zeros((128, 40), np.float32)
    gc[:, 0:8] = f("ffn1_norm")[0].reshape(8, 128).T
    gc[:, 8:16] = f("mix_norm")[0].reshape(8, 128).T
    gc[:, 16:24] = f("ffn2_norm")[0].reshape(8, 128).T
    gc[:, 24:28] = f("attn_out_norm")[0].reshape(4, 128).T
    gc[:, 28] = f("attn_q_norm")[0]
    gc[:, 29] = f("attn_k_norm")[0]
    gc[:, 30] = f("rec_out_norm")[0]
    lb = f("rec_lb_logits")
    lbl = np.ascontiguousarray(lb.reshape(2, 2, 4, 128).transpose(3, 0, 1, 2).reshape(128, 16))
    qkrow = np.concatenate([f("attn_q_norm")[0], f("attn_k_norm")[0]])[None, :]
    shared = {
        "ffn1_w_in": f("ffn1_w_in")[0], "ffn1_w_out": f("ffn1_w_out")[0],
        "ffn2_w_in": f("ffn2_w_in")[0], "ffn2_w_out": f("ffn2_w_out")[0],
        "w_in_mix": f("w_in_mix")[0], "w_out_mix": f("w_out_mix")[0],
        "gcols": gc, "final_norm": f("final_norm"), "lbl": lbl, "qkrow": np.ascontiguousarray(qkrow),
        "ident": np.eye(128, dtype=np.float32),
    }
    maps = []
    for c in range(NCORES):
        b, r = c // 4, c % 4
        cosT, sinT, rot, mask, chain = _consts(c)
        m = dict(shared)
        m["x"] = np.ascontiguousarray(x[b, r * T:(r + 1) * T, :])
        m.update({"ropecos": cosT, "ropesin": sinT, "rotm": rot, "trimask": mask, "chain": chain})
        maps.append(m)
    return maps


def kernel(**inputs):
    debug = bool(int(os.environ.get("MK_DEBUG", "0")))
    stop_after = os.environ.get("MK_STOP") or None
    key = (debug, stop_after)
    if key not in _NC_CACHE:
        _NC_CACHE[key] = build(debug=debug, stop_after=stop_after)
    nc = _NC_CACHE[key]
    maps = make_in_maps(inputs)
    res = run_bass_kernel_spmd(nc, maps, core_ids=list(range(NCORES)))
    out = np.zeros((2, 8192, D), np.float32)
    for c in range(NCORES):
        b, r = c // 4, c % 4
        out[b, r * T:(r + 1) * T, :] = res.results[c]["out"]
    if debug:
        kernel.last_debug = res.results
    return out
```
